# Optimizing a Trainium2 kernel written in Bass

```python
import jax, jax.numpy as jnp
from jax import lax
import numpy as np

D_MODEL = 2048
BATCH = 4
SEQ = 4096
DEPTH = 1

GRID_W = 64
CTX_LEN = 256
D_INNER = D_MODEL
W_A = D_INNER // 2
W_B = D_INNER - W_A
HG_HEADS = 8
HG_DK = W_A // HG_HEADS
HG_DV = W_A // HG_HEADS
ML_HEADS = 4
ML_DH = W_B // ML_HEADS
CHUNK = 64
CONV_K = 3
N_IN = 5 * W_A + 5 * W_B + 4 * ML_HEADS
ALPHA = (2 * DEPTH) ** 0.25
BETA = (8 * DEPTH) ** -0.25
LN_EPS = 1e-5
NORM_EPS = 1e-6

kernel_name = "hymba_hgrn2_mlstm_bidir_dit_block"


def layer_norm(a, g, b):
    af = a.astype(jnp.float32)
    mu = jnp.mean(af, axis=-1, keepdims=True)
    var = jnp.mean(jnp.square(af - mu), axis=-1, keepdims=True)
    out = (af - mu) * lax.rsqrt(var + LN_EPS) * g.astype(jnp.float32) + b.astype(jnp.float32)
    return out.astype(a.dtype)


def modulate(a, shift, scale):
    af = a.astype(jnp.float32)
    mu = jnp.mean(af, axis=-1, keepdims=True)
    var = jnp.mean(jnp.square(af - mu), axis=-1, keepdims=True)
    n = (af - mu) * lax.rsqrt(var + LN_EPS)
    return (n * (1.0 + scale.astype(jnp.float32)) + shift.astype(jnp.float32)).astype(a.dtype)


def rms_norm(a):
    return a * lax.rsqrt(jnp.mean(jnp.square(a), axis=-1, keepdims=True) + NORM_EPS)


def head_layer_norm(a):
    mu = jnp.mean(a, axis=-1, keepdims=True)
    var = jnp.mean(jnp.square(a - mu), axis=-1, keepdims=True)
    return (a - mu) * lax.rsqrt(var + NORM_EPS)


def heads(a, n_heads):
    return a.reshape(a.shape[:2] + (n_heads, a.shape[-1] // n_heads))


def flip(a):
    return jnp.flip(a, axis=1)


def to_chunks(a):
    bsz, t = a.shape[:2]
    a = a.reshape((bsz, t // CHUNK, CHUNK) + a.shape[2:])
    return jnp.swapaxes(jnp.moveaxis(a, 1, 0), 2, 3)


def from_chunks(o):
    o = jnp.moveaxis(jnp.swapaxes(o, 2, 3), 0, 1)
    return o.reshape((o.shape[0], o.shape[1] * o.shape[2]) + o.shape[3:])


def hgrn2_scan(q, k, v, logf, s0):
    mask = jnp.tril(jnp.ones((CHUNK, CHUNK), dtype=bool))[:, :, None]

    def step(s, inp):
        qc, kc, vc, gc = inp
        b = jnp.cumsum(gc, axis=2)
        o_inter = jnp.einsum('bhtk,bhkv->bhtv', qc * jnp.exp(b), s)
        diff = b[:, :, :, None, :] - b[:, :, None, :, :]
        decay = jnp.exp(jnp.where(mask, diff, -jnp.inf))
        scores = jnp.einsum('bhtk,bhsk,bhtsk->bhts', qc, kc, decay)
        o = o_inter + jnp.einsum('bhts,bhsv->bhtv', scores, vc)
        b_last = b[:, :, -1]
        k_dec = kc * jnp.exp(b_last[:, :, None, :] - b)
        s_new = jnp.exp(b_last)[..., None] * s + jnp.einsum('bhsk,bhsv->bhkv', k_dec, vc)
        return s_new, o

    s_fin, o = lax.scan(step, s0, (to_chunks(q), to_chunks(k), to_chunks(v), to_chunks(logf)))
    return from_chunks(o), s_fin


def mlstm_scan(q, k, v, log_i, log_f, state):
    mask = jnp.tril(jnp.ones((CHUNK, CHUNK), dtype=bool))

    def step(carry, inp):
        c_mat, n_vec, m = carry
        qc, kc, vc, ic, fc = inp
        b = jnp.cumsum(fc, axis=-1)
        log_w = jnp.where(mask, b[..., :, None] - b[..., None, :] + ic[..., None, :], -jnp.inf)
        m_inter = b + m[..., None]
        m_t = jnp.maximum(m_inter, jnp.max(log_w, axis=-1))
        w_inter = jnp.exp(m_inter - m_t)
        w_qk = jnp.exp(log_w - m_t[..., None]) * jnp.einsum('bhtk,bhsk->bhts', qc, kc)
        num = (w_inter[..., None] * jnp.einsum('bhvk,bhtk->bhtv', c_mat, qc)
               + jnp.einsum('bhts,bhsv->bhtv', w_qk, vc))
        den = w_inter * jnp.einsum('bhk,bhtk->bht', n_vec, qc) + jnp.sum(w_qk, axis=-1)
        h = num / jnp.maximum(jnp.abs(den), jnp.exp(-m_t))[..., None]
        m_new = m_t[..., -1]
        w_s = jnp.exp(b[..., -1:] - b + ic - m_new[..., None])
        decay = jnp.exp(b[..., -1] + m - m_new)
        c_new = decay[..., None, None] * c_mat + jnp.einsum('bhsv,bhsk->bhvk', w_s[..., None] * vc, kc)
        n_new = decay[..., None] * n_vec + jnp.einsum('bhs,bhsk->bhk', w_s, kc)
        return (c_new, n_new, m_new), h

    st, h = lax.scan(step, state, (to_chunks(q), to_chunks(k), to_chunks(v), to_chunks(log_i), to_chunks(log_f)))
    return from_chunks(h), st


def short_conv(a, w, b, grid):
    ch = a.shape[-1]
    w = w.astype(jnp.float32)
    if grid:
        rows = a.shape[1] // GRID_W
        img = a.reshape(a.shape[0], rows, GRID_W, ch)
        out = lax.conv_general_dilated(img, w[:, :, None, :], (1, 1), 'SAME',
                                       dimension_numbers=('NHWC', 'HWIO', 'NHWC'), feature_group_count=ch)
        out = out.reshape(a.shape)
    else:
        out = lax.conv_general_dilated(a, w[1][:, None, :], (1,), 'SAME',
                                       dimension_numbers=('NWC', 'WIO', 'NWC'), feature_group_count=ch)
    return jax.nn.silu(out + b.astype(jnp.float32))


def hgrn2_gates(z, lb):
    f = lb + (1.0 - lb) * jax.nn.sigmoid(z)
    return jnp.log(f), 1.0 - f


def zero_states(bsz):
    hg = jnp.zeros((bsz, HG_HEADS, HG_DK, HG_DV), jnp.float32)
    ml = (jnp.zeros((bsz, ML_HEADS, ML_DH, ML_DH), jnp.float32),
          jnp.zeros((bsz, ML_HEADS, ML_DH), jnp.float32),
          jnp.zeros((bsz, ML_HEADS), jnp.float32))
    return (hg, hg, ml, ml)


def mix(u, grid, states, lb, conv_w_l, conv_b_l, gate_b_l, hg_norm_l, ml_norm_l):
    u = u.astype(jnp.float32)
    bsz, t = u.shape[:2]
    splits = [W_A, 2 * W_A, 3 * W_A, 4 * W_A, 5 * W_A,
              5 * W_A + 2 * W_B, 5 * W_A + 3 * W_B, 5 * W_A + 4 * W_B, 5 * W_A + 5 * W_B]
    a_q, a_ff, a_fb, a_i, a_z, b_qk, b_v, b_o, b_z, b_g = jnp.split(u, splits, axis=-1)
    hg_f0, hg_b0, ml_f0, ml_b0 = states

    q_a = heads(jax.nn.silu(a_q), HG_HEADS)
    v_a = heads(a_i, HG_HEADS)
    logf_f, k_f = hgrn2_gates(a_ff, lb[0])
    logf_b, k_b = hgrn2_gates(a_fb, lb[1])
    o_f, hg_f = hgrn2_scan(q_a, heads(k_f, HG_HEADS), v_a, heads(logf_f, HG_HEADS), hg_f0)
    o_b, hg_b = hgrn2_scan(flip(q_a), flip(heads(k_b, HG_HEADS)), flip(v_a), flip(heads(logf_b, HG_HEADS)), hg_b0)
    o_a = rms_norm(o_f + flip(o_b)) * hg_norm_l.astype(jnp.float32).reshape(HG_HEADS, HG_DV)
    y_a = o_a.reshape(bsz, t, W_A) * jax.nn.silu(a_z)

    qk = short_conv(b_qk, conv_w_l, conv_b_l, grid)
    q_m, k_m = jnp.split(qk, 2, axis=-1)
    q_m = heads(q_m, ML_HEADS)
    k_m = heads(k_m, ML_HEADS) * (ML_DH ** -0.5)
    v_m = heads(b_v, ML_HEADS)
    g = b_g.reshape(bsz, t, 4, ML_HEADS) + gate_b_l.astype(jnp.float32)
    log_i_f, log_i_b = g[:, :, 0], g[:, :, 1]
    log_f_f, log_f_b = jax.nn.log_sigmoid(g[:, :, 2]), jax.nn.log_sigmoid(g[:, :, 3])
    h_f, ml_f = mlstm_scan(q_m, k_m, v_m, log_i_f, log_f_f, ml_f0)
    h_b, ml_b = mlstm_scan(flip(q_m), flip(k_m), flip(v_m), flip(log_i_b), flip(log_f_b), ml_b0)
    h = head_layer_norm(h_f + flip(h_b)) * ml_norm_l.astype(jnp.float32).reshape(ML_HEADS, ML_DH)
    y_b = h.reshape(bsz, t, W_B) * jax.nn.sigmoid(b_o) * jax.nn.silu(b_z)

    return jnp.concatenate([y_a, y_b], axis=-1), (hg_f, hg_b, ml_f, ml_b)


def setup_inputs(seed: int = 0) -> dict:
    key = jax.random.key(seed)
    ks = jax.random.split(key, 20)
    f32 = jnp.float32
    x = jax.random.normal(ks[0], (BATCH, SEQ, D_MODEL), f32)
    c = jax.random.normal(ks[1], (BATCH, D_MODEL), f32)
    ctx = jax.random.normal(ks[2], (BATCH, CTX_LEN, D_MODEL), f32)
    c_ctx = jax.random.normal(ks[3], (D_MODEL,), f32)
    w_mod = jax.random.normal(ks[4], (DEPTH, D_MODEL, 3 * D_MODEL), f32) * (0.5 * D_MODEL ** -0.5)
    b_mod = jax.random.normal(ks[5], (DEPTH, 3 * D_MODEL), f32) * 0.02
    w_in = jax.random.normal(ks[6], (DEPTH, D_MODEL, N_IN), f32) * (D_MODEL ** -0.5)
    conv_w = jax.random.normal(ks[7], (DEPTH, CONV_K, CONV_K, 2 * W_B), f32) * (1.0 / CONV_K)
    conv_b = jax.random.normal(ks[8], (DEPTH, 2 * W_B), f32) * 0.02
    hg_lb = jax.random.normal(ks[9], (2, DEPTH + 1, W_A), f32) * 0.1
    ig_b = jax.random.normal(ks[10], (DEPTH, 2, ML_HEADS), f32) * 0.1
    fg_b = jnp.linspace(3.0, 6.0, ML_HEADS, dtype=f32)[None, None, :] + 0.1 * jax.random.normal(ks[11], (DEPTH, 2, ML_HEADS), f32)
    ml_gate_b = jnp.concatenate([ig_b, fg_b], axis=1)
    hg_norm_w = 1.0 + 0.02 * jax.random.normal(ks[12], (DEPTH, W_A), f32)
    ml_norm_w = 1.0 + 0.02 * jax.random.normal(ks[13], (DEPTH, W_B), f32)
    w_out = jax.random.normal(ks[14], (DEPTH, D_INNER, D_MODEL), f32) * (BETA * D_INNER ** -0.5)
    ln_g = 1.0 + 0.02 * jax.random.normal(ks[15], (DEPTH, D_MODEL), f32)
    ln_b = 0.02 * jax.random.normal(ks[16], (DEPTH, D_MODEL), f32)
    return {"x": x, "c": c, "ctx": ctx, "c_ctx": c_ctx, "w_mod": w_mod, "b_mod": b_mod, "w_in": w_in,
            "conv_w": conv_w, "conv_b": conv_b, "hg_lb": hg_lb, "ml_gate_b": ml_gate_b,
            "hg_norm_w": hg_norm_w, "ml_norm_w": ml_norm_w, "w_out": w_out, "ln_g": ln_g, "ln_b": ln_b}


def reference(x, c, ctx, c_ctx, w_mod, b_mod, w_in, conv_w, conv_b, hg_lb, ml_gate_b,
              hg_norm_w, ml_norm_w, w_out, ln_g, ln_b):
    lower = jnp.cumsum(jax.nn.softmax(hg_lb.astype(jnp.float32), axis=1), axis=1)
    states0 = zero_states(ctx.shape[0])
    for layer in range(DEPTH):
        mod_x = jax.nn.silu(c) @ w_mod[layer] + b_mod[layer]
        mod_c = jax.nn.silu(c_ctx) @ w_mod[layer] + b_mod[layer]
        shift_x, scale_x, gate_x = jnp.split(mod_x[:, None, :], 3, axis=-1)
        shift_c, scale_c, gate_c = jnp.split(mod_c, 3, axis=-1)
        lp = (lower[:, layer], conv_w[layer], conv_b[layer], ml_gate_b[layer], hg_norm_w[layer], ml_norm_w[layer])
        u_ctx = modulate(ctx, shift_c, scale_c) @ w_in[layer]
        y_ctx, ctx_states = mix(u_ctx, False, states0, *lp)
        u_x = modulate(x, shift_x, scale_x) @ w_in[layer]
        y_x, _ = mix(u_x, True, ctx_states, *lp)
        x = layer_norm(ALPHA * x + gate_x * (y_x.astype(x.dtype) @ w_out[layer]), ln_g[layer], ln_b[layer])
        if layer < DEPTH - 1:
            ctx = layer_norm(ALPHA * ctx + gate_c * (y_ctx.astype(ctx.dtype) @ w_out[layer]), ln_g[layer], ln_b[layer])
    return x
```

```python
import math
from contextlib import ExitStack
import numpy as np
import concourse.bass as bass
import concourse.mybir as mybir
from concourse.bass_utils import run_bass_kernel_spmd

F32 = mybir.dt.float32
BF16 = mybir.dt.bfloat16
ALU = mybir.AluOpType
AF = mybir.ActivationFunctionType

D = 2048
T_CTX = 256
T_LAT = 4096
T_ALL = T_CTX + T_LAT
NT = T_ALL // 128
NT_CTX = T_CTX // 128
NBLK = (T_ALL + 511) // 512
T_OWN = 2048
OWN0, OWN1 = NT_CTX, NT_CTX + T_OWN // 128
NT_OWN = T_OWN // 128
T_FWD = 2560
NB_OWN = 5
HG_H = 8
ML_H = 4
LN_EPS = 1e-5
NORM_EPS = 1e-6
ALPHA = 2.0 ** 0.25
N_CORES = 8


class Buf:
    __slots__ = ("name", "writer", "readers", "excl")

    def __init__(self, name):
        self.name = name
        self.writer = None
        self.readers = {}
        self.excl = False


class Sched:
    def __init__(self, nc, stack):
        self.nc = nc
        self.stack = stack
        self.eng = {"pe": nc.tensor, "act": nc.scalar, "dve": nc.vector, "pool": nc.gpsimd, "sp": nc.sync}
        self.sem, self.cnt, self.waited = {}, {}, {}
        for e in self.eng:
            self.sem[e] = stack.enter_context(nc.semaphore("sem_" + e))
            self.cnt[e] = 0
            self.waited[e] = {}
        self.dsem, self.dcnt = {}, {}
        self.nb = 0

    def buf(self, name=None):
        self.nb += 1
        return Buf(name or f"b{self.nb}")

    def bufs(self, n, name="r"):
        return [self.buf(f"{name}{i}") for i in range(n)]

    def _q(self, q):
        if q not in self.dsem:
            self.dsem[q] = self.stack.enter_context(self.nc.semaphore("dq_" + q))
            self.dcnt[q] = 0

    def _wait(self, e, dep):
        src, idx = dep
        if src == e and e == "pe":
            return
        w = self.waited[e]
        if w.get(src, 0) >= idx:
            return
        w[src] = idx
        self.eng[e].wait_ge(self.sem[src] if src in self.sem else self.dsem[src], idx)

    def _deps(self, e, reads, writes):
        for b in reads:
            if b.writer is not None:
                self._wait(e, b.writer)
            if b.excl:
                for src, idx in list(b.readers.items()):
                    if src != e:
                        self._wait(e, (src, idx))
        for b in writes:
            if b.writer is not None:
                self._wait(e, b.writer)
            for src, idx in list(b.readers.items()):
                self._wait(e, (src, idx))

    def _mark(self, me, reads, writes):
        for b in reads:
            if b.readers.get(me[0], 0) < me[1]:
                b.readers[me[0]] = me[1]
        for b in writes:
            b.writer = me
            b.readers = {}

    def op(self, e, fn, reads=(), writes=()):
        self._deps(e, reads, writes)
        inst = fn(self.eng[e])
        self.cnt[e] += 1
        inst.then_inc(self.sem[e], 1)
        self._mark((e, self.cnt[e]), reads, writes)
        return inst

    def dma(self, e, q, out, in_, reads=(), writes=(), **kw):
        self._q(q)
        self._deps(e, reads, writes)
        if self.dcnt[q] > 0:
            self._wait(e, (q, self.dcnt[q]))
        inst = self.eng[e].dma_start(out=out, in_=in_, **kw)
        self.dcnt[q] += 16
        inst.then_inc(self.dsem[q], 16)
        self._mark((q, self.dcnt[q]), reads, writes)
        return inst

    def barrier(self):
        for e in self.eng:
            self.wait_all(e)

    def wait_all(self, e):
        for src in list(self.sem):
            if self.cnt[src] > 0:
                self._wait(e, (src, self.cnt[src]))
        for q in list(self.dsem):
            if self.dcnt[q] > 0:
                self._wait(e, (q, self.dcnt[q]))


class Rot:
    def __init__(self, S, views, name, bufs=None):
        self.views = views
        self.bufs = bufs if bufs is not None else S.bufs(len(views), name)
        self.i = 0

    def next(self):
        k = self.i % len(self.views)
        self.i += 1
        return self.views[k], self.bufs[k]


def build_program(debug=False, stop_after=99):
    nc = bass.Bass("TRN2", target_bir_lowering=False)
    dt_in = lambda name, shape: nc.dram_tensor(name, shape, F32, kind="ExternalInput").ap()
    xt = dt_in("xt", [T_ALL, D])
    cvec = dt_in("cvec", [128, 2, 16])
    wmod = dt_in("wmod", [12, 128, 16, 512])
    bmod = dt_in("bmod", [1, 6144])
    bmodfm = dt_in("bmodfm", [128, 48])
    win = dt_in("win", [80, 128, 16, 128])
    wgate = dt_in("wgate", [128, 16, 16])
    convw = dt_in("convw", [128, 16, 9])
    convb = dt_in("convb", [128, 16])
    hglb = dt_in("hglb", [128, 2, 2, 8])
    gateb = dt_in("gateb", [16, 1])
    hgnw = dt_in("hgnw", [128, 8])
    mlnw = dt_in("mlnw", [128, 8])
    wout = dt_in("wout", [128, 16, D])
    lng = dt_in("lng", [1, D])
    lnb = dt_in("lnb", [1, D])
    out = nc.dram_tensor("out", [T_OWN, D], F32, kind="ExternalOutput").ap()
    XM = nc.dram_tensor("xm_scr", [NBLK, 128, 4, 16, 128], BF16, **({"kind": "ExternalOutput"} if debug else {})).ap()
    Y = nc.dram_tensor("y_scr", [NT_OWN, 128, 16, 128], BF16, **({"kind": "ExternalOutput"} if debug else {})).ap()

    with ExitStack() as st:
        S = Sched(nc, st)
        sb = lambda name, shape, dt=F32, stack=st: stack.enter_context(nc.sbuf_tensor(name, shape, dt))
        ps = lambda name, shape, dt=F32, stack=st: stack.enter_context(nc.psum_tensor(name, shape, dt))

        ident = sb("ident", [128, 128], BF16)
        maskf = sb("maskf", [128, 128], F32)
        maskb = sb("maskb", [128, 128], F32)
        gatex = sb("gatex", [128, 2048], F32)
        hgA = sb("hgA", [128, 2, 8], F32)
        hgB = sb("hgB", [128, 2, 8], F32)
        hgnBm = sb("hgnBm", [128, 2, 8], F32)
        hgn = sb("hgn", [128, 8], F32)
        mln = sb("mln", [128, 8], F32)
        cw = sb("cw", [128, 16, 9], F32)
        cb = sb("cb", [128, 16], F32)
        gb = sb("gb", [16, 1], F32)
        b_const = S.buf("const")
        b_mod = S.buf("mod")
        msf = sb("msf", [128, 32, 2], F32)

        banks = [ps(f"bk{i}", [128, 512], F32) for i in range(6)]
        TBs = [ps(f"tbk{i}", [128, 1024], BF16) for i in range(2)]
        bbuf = S.bufs(6, "bank")
        tbuf = S.bufs(2, "tbank")
        for b_ in bbuf + tbuf:
            b_.excl = True
        rA = Rot(S, [banks[0], banks[1]], "pA", bufs=[bbuf[0], bbuf[1]])
        rT = Rot(S, [TBs[0], TBs[1]], "pT", bufs=[tbuf[0], tbuf[1]])
        rA6g = Rot(S, banks, "pA6g", bufs=bbuf)

        S.op("pool", lambda e: e.memset(ident[:], 1.0), writes=[b_const])
        S.op("pool", lambda e: e.affine_select(ident[:], ident[:], [[-1, 128]], ALU.is_equal, 0.0, base=0, channel_multiplier=1),
             reads=[b_const], writes=[b_const])
        S.op("pool", lambda e: e.memset(maskf[:], 1.0), writes=[b_const])
        S.op("pool", lambda e: e.affine_select(maskf[:], maskf[:], [[1, 128]], ALU.is_ge, 0.0, base=0, channel_multiplier=-1),
             reads=[b_const], writes=[b_const])
        S.op("pool", lambda e: e.memset(maskb[:], 1.0), writes=[b_const])
        S.op("pool", lambda e: e.affine_select(maskb[:], maskb[:], [[-1, 128]], ALU.is_ge, 0.0, base=0, channel_multiplier=1),
             reads=[b_const], writes=[b_const])
        for dst, src, q in ((hgn, hgnw, "c0"), (mln, mlnw, "c1"), (cw, convw, "c2"), (cb, convb, "c3"), (gb, gateb, "c4")):
            S.dma("sp", q, dst[:], src, writes=[b_const])

        with ExitStack() as s0:
            lbt = sb("lbt", [128, 2, 2, 8], F32, s0)
            lbe = sb("lbe", [128, 2, 2, 8], F32, s0)
            lbs = sb("lbs", [128, 2, 8], F32, s0)
            lbv = sb("lbv", [128, 2, 8], F32, s0)
            bt = S.buf("lbtmp")
            S.dma("sp", "c5", lbt[:], hglb, writes=[bt])
            S.op("act", lambda e: e.activation(lbe[:], lbt[:], AF.Exp), reads=[bt], writes=[bt])
            S.op("dve", lambda e: e.tensor_tensor(lbs[:], lbe[:, :, 0, :], lbe[:, :, 1, :], ALU.add), reads=[bt], writes=[bt])
            S.op("dve", lambda e: e.reciprocal(lbs[:], lbs[:]), reads=[bt], writes=[bt])
            S.op("dve", lambda e: e.tensor_tensor(lbv[:], lbe[:, :, 0, :], lbs[:], ALU.mult), reads=[bt], writes=[bt])
            S.op("dve", lambda e: e.tensor_scalar(hgA[:], lbv[:], 0.5, 0.5, ALU.mult, ALU.add), reads=[bt], writes=[b_const])
            S.op("dve", lambda e: e.tensor_scalar(hgB[:], lbv[:], -0.5, 0.5, ALU.mult, ALU.add), reads=[bt], writes=[b_const])
            S.op("dve", lambda e: e.tensor_scalar(hgnBm[:], lbv[:], 0.5, -0.5, ALU.mult, ALU.add), reads=[bt], writes=[b_const])

        S.barrier()
        s_p1 = ExitStack()
        with ExitStack() as s0:
            cv = sb("cv", [128, 2, 16], F32, s_p1)
            csil = sb("csil", [128, 2, 16], F32, s_p1)
            csb = sb("csb", [128, 16, 2], BF16, s_p1)
            crep = sb("crep", [128, 16, 128], BF16, s_p1)
            bmb = sb("bmb", [128, 2048], F32, s_p1)
            bfm = sb("bfm", [128, 48], F32, s_p1)
            wmb = [sb(f"wmb{i}", [128, 16, 512], BF16, s_p1) for i in range(2)]
            rW = Rot(S, wmb, "wmb")
            bc = S.buf("cvt")
            S.dma("sp", "c6", cv[:], cvec, writes=[bc])
            S.dma("sp", "c7", bmb[:], bmod[:, 4096:6144].partition_broadcast(128), writes=[bc])
            S.dma("sp", "c8", bfm[:], bmodfm, writes=[bc])
            S.op("act", lambda e: e.activation(csil[:], cv[:], AF.Silu), reads=[bc], writes=[bc])
            S.op("dve", lambda e: e.tensor_copy(csb[:], csil[:].rearrange("p j c -> p c j")), reads=[bc], writes=[bc])
            for c in range(16):
                S.op("dve", lambda e: e.tensor_scalar(crep[:, c, :], maskf[:], 0.0, csil[:, 0, c:c + 1], ALU.mult, ALU.add),
                     reads=[bc, b_const], writes=[bc])
            for blk in range(12):
                wv, wbuf = rW.next()
                S.dma("pool", f"wm{blk % 2}", wv[:], wmod[blk], writes=[wbuf], max_dma_last_dim=4096)
                if blk < 8:
                    for q in range(4):
                        cbk = blk * 4 + q
                        pv, pb = rA.next()
                        for c in range(16):
                            S.op("pe", lambda e: e.matmul(pv[:, 0:2], wv[:, c, q * 128:(q + 1) * 128], csb[:, c, :], start=(c == 0), stop=(c == 15)),
                                 reads=[bc, wbuf], writes=[pb])
                        S.op("dve", lambda e: e.tensor_scalar(msf[:, cbk, :], pv[:, 0:2], bfm[:, cbk:cbk + 1], (1.0 if cbk >= 16 else 0.0), ALU.add, ALU.add),
                             reads=[pb, bc], writes=[b_mod])
                else:
                    pv, pb = rA.next()
                    for c in range(16):
                        S.op("pe", lambda e: e.matmul(pv[:], crep[:, c, :], wv[:, c, :], start=(c == 0), stop=(c == 15)),
                             reads=[bc, wbuf], writes=[pb])
                    S.op("dve", lambda e: e.tensor_tensor(gatex[:, (blk - 8) * 512:(blk - 7) * 512], pv[:], bmb[:, (blk - 8) * 512:(blk - 7) * 512], ALU.add),
                         reads=[pb, bc], writes=[b_mod])

        b_xm = S.buf("XM")
        with ExitStack() as s0:
            xin = [sb(f"xin{i}", [128, D], F32, s0) for i in range(3)]
            xmb = [sb(f"xmb{i}", [128, D], BF16, s0) for i in range(3)]
            xmt = [sb(f"xmt{i}", [128, 16, 128], BF16, s0) for i in range(2)]
            stt = [sb(f"stt{i}", [128, 4, 6], F32, s0) for i in range(3)]
            mv = [sb(f"mv{i}", [128, 4], F32, s0) for i in range(3)]
            rX, rXb, rXt, rSt, rMv = Rot(S, xin, "xin"), Rot(S, xmb, "xmb"), Rot(S, xmt, "xmt"), Rot(S, stt, "stt"), Rot(S, mv, "mv")
            xin_q = []

            def load_x(t):
                xv, xb_ = rX.next()
                S.dma("sp", f"xin{t % 3}", xv[:], xt[t * 128:(t + 1) * 128, :], writes=[xb_])
                xin_q.append((xv, xb_))

            load_x(0)
            load_x(1)
            pend = {}
            xt_chunk_bufs = {id(b_): S.bufs(16, "xtc") for b_ in rXt.bufs}

            def stage_a(t):
                if t + 2 < NT:
                    load_x(t + 2)
                xv, xb_ = xin_q.pop(0)
                sv, sb_ = rSt.next()
                for i in range(4):
                    S.op("dve", lambda e: e.bn_stats(sv[:, i, :], xv[:, i * 512:(i + 1) * 512]), reads=[xb_], writes=[sb_])
                mvv, mvb = rMv.next()
                S.op("dve", lambda e: e.bn_aggr(mvv[:, 0:2], sv[:].rearrange("p a b -> p (a b)")), reads=[sb_], writes=[mvb])
                S.op("act", lambda e: e.activation(mvv[:, 2:3], mvv[:, 1:2], AF.Ln, bias=LN_EPS, scale=1.0), reads=[mvb], writes=[mvb])
                S.op("act", lambda e: e.activation(mvv[:, 3:4], mvv[:, 2:3], AF.Exp, scale=-0.5), reads=[mvb], writes=[mvb])
                xbv, xbb = rXb.next()
                S.op("dve", lambda e: e.tensor_scalar(xbv[:], xv[:], mvv[:, 0:1], mvv[:, 3:4], ALU.subtract, ALU.mult), reads=[xb_, mvb], writes=[xbb])
                pend[t] = (xbv, xbb)

            def stage_b(t):
                xbv, xbb = pend.pop(t)
                jm = 1 if t < NT_CTX else 0
                xtv, xtb = rXt.next()
                cb_ = xt_chunk_bufs[id(xtb)]
                for g in range(2):
                    tv, tb = rT.next()
                    for i in range(8):
                        c = g * 8 + i
                        S.op("pe", lambda e: e.transpose(tv[:, i * 128:(i + 1) * 128], xbv[:, c * 128:(c + 1) * 128], ident[:]),
                             reads=[xbb, b_const], writes=[tb])
                    for i in range(8):
                        c = g * 8 + i
                        if g == 1:
                            S.op("dve", lambda e: e.tensor_scalar(xtv[:, c, :], tv[:, i * 128:(i + 1) * 128], msf[:, 16 + c, jm:jm + 1], msf[:, c, jm:jm + 1], ALU.mult, ALU.add),
                                 reads=[tb, b_mod], writes=[cb_[c]])
                        else:
                            S.op("act", lambda e: e.activation(xtv[:, c, :], tv[:, i * 128:(i + 1) * 128], AF.Identity, bias=msf[:, c, jm:jm + 1], scale=msf[:, 16 + c, jm:jm + 1]),
                                 reads=[tb, b_mod], writes=[cb_[c]])
                S.dma("sp", f"xmo{t % 2}", XM[t // 4][:, t % 4, :, :], xtv[:], reads=cb_, writes=[b_xm])

            stage_a(0)
            for t in range(NT):
                if t + 1 < NT:
                    stage_a(t + 1)
                stage_b(t)
        S.barrier()
        s_p1.close()
        if stop_after <= 1:
            S.wait_all("sp")
            return nc
        b_y = S.buf("Y")

        def inproj(blocks_spec, wtile, wbuf, xblk_rot, evac, nblk_of=None):
            nblk_of = nblk_of or {}
            nmax = max(nblk_of.get(sl_, NBLK) for sl_ in blocks_spec)
            for blk in range(nmax):
                ntok = min(512, T_ALL - blk * 512)
                xv, xb_ = xblk_rot.next()
                S.dma("sp", f"xblk{blk % 2}", xv[:, 0:ntok // 128, :, :], XM[blk][:, 0:ntok // 128, :, :], reads=[b_xm], writes=[xb_])
                for slot in blocks_spec:
                    if blk >= nblk_of.get(slot, NBLK):
                        continue
                    pv, pb = rA6g.next()
                    for c in range(16):
                        S.op("pe", lambda e: e.matmul(pv[:, 0:ntok], wtile[:, slot, c, :], xv[:, 0:ntok // 128, c, :], start=(c == 0), stop=(c == 15)),
                             reads=[wbuf, xb_], writes=[pb])
                    evac(slot, blk, ntok, pv, pb)

        S.barrier()
        with ExitStack() as s0:
            wt = sb("hw", [128, 5, 16, 128], BF16, s0)
            wtb = S.buf("hw")
            xblk = [sb(f"hx{i}", [128, 4, 16, 128], BF16, s0) for i in range(2)]
            rXB = Rot(S, xblk, "hx")
            Q = sb("hQ", [128, T_ALL], BF16, s0)
            Bc = [sb(f"hBc{d}", [128, T_ALL], F32, s0) for d in range(2)]
            Kd = [sb(f"hK{d}", [128, T_ALL], BF16, s0) for d in range(2)]
            Vf = sb("hV", [128, T_ALL], BF16, s0)
            Vt = sb("hVt", [128, NT, 128], BF16, s0)
            Z = sb("hZ", [128, T_OWN], BF16, s0)
            Of = sb("hOf", [128, NT_OWN, 128], BF16, s0)
            LFt = sb("hLF", [128, T_ALL], F32, s0)
            tmp = [sb(f"htmp{i}", [128, 512], F32, s0) for i in range(2)]
            rTmp = Rot(S, tmp, "htmp")
            small = sb("hsm", [128, NT, 6], F32, s0)
            St = sb("hS", [128, 128], F32, s0)
            mk = lambda nm, shp, dt, n=2: Rot(S, [sb(f"{nm}{i}", shp, dt, s0) for i in range(n)], nm)
            rSr, rOn, rYt, rKdt, rKt, rKD = (mk(nm, [128, 128], BF16, 3) for nm in ("hSr", "hon", "hyt", "hkdt", "hKt", "hKD"))
            rP = mk("hP", [128, 128], BF16, 6)
            rQt = mk("hQt", [128, 128], BF16, 8)
            rS4 = Rot(S, [banks[0][:, 0:128], banks[1][:, 0:128]], "hS", bufs=[bbuf[0], bbuf[1]])
            rD8 = Rot(S, [banks[2 + (k % 2)][:, (k // 2) * 128:(k // 2 + 1) * 128] for k in range(8)], "hD", bufs=[bbuf[2 + (k % 2)] for k in range(8)])
            rO8 = Rot(S, [banks[4][:, 0:128], banks[5][:, 0:128]], "hO", bufs=[bbuf[4], bbuf[5]])
            rTk = Rot(S, [TBs[0][:, 0:128]], "hTk", bufs=[tbuf[0]])
            rTf = Rot(S, [TBs[1][:, 0:128]], "hTf", bufs=[tbuf[1]])
            rTv = Rot(S, [TBs[0][:, 0:128], TBs[1][:, 0:128]], "hTv", bufs=[tbuf[0], tbuf[1]])
            rXa, rXr, rXt = (mk(nm, [128, 128], F32, 3) for nm in ("hXa", "hXr", "hXt"))
            rE1, rE2, rE3 = (mk(nm, [128, 128], BF16, 3) for nm in ("hE1", "hE2", "hE3"))
            rOs = mk("hos", [128, 128], F32, 7)
            rFin = mk("hfin", [128, 4], F32, 7)
            junk = sb("hjunk", [128, 128], F32, s0)
            bQ, bV, bVt, bZ, bOf, bLF, bS, bJ = (S.buf(n) for n in ("hQ", "hV", "hVt", "hZ", "hOf", "hLF", "hS", "hj"))
            bSmT = S.bufs(NT, "hsm")
            bBc = S.bufs(2, "hBc")
            bK = S.bufs(2, "hK")

            for h in range(HG_H):
                for sl_ in range(5):
                    S.dma("pool", "hw", wt[:, sl_, :, :], win[h * 5 + sl_], writes=[wtb], max_dma_last_dim=4096)

                def evac(slot, blk, ntok, pv, pb, h=h):
                    lo = blk * 512
                    if slot == 0:
                        S.op("act", lambda e: e.activation(Q[:, lo:lo + ntok], pv[:, 0:ntok], AF.Silu), reads=[pb], writes=[bQ])
                    elif slot in (1, 2):
                        d = slot - 1
                        tv, tb = rTmp.next()
                        S.op("act", lambda e: e.activation(tv[:, 0:ntok], pv[:, 0:ntok], AF.Tanh, scale=0.5), reads=[pb], writes=[tb])
                        S.op("dve", lambda e: e.tensor_scalar(Kd[d][:, lo:lo + ntok], tv[:, 0:ntok], hgnBm[:, d, h:h + 1], hgB[:, d, h:h + 1], ALU.mult, ALU.add),
                             reads=[tb, b_const], writes=[bK[d]])
                        S.op("dve", lambda e: e.tensor_scalar(Bc[d][:, lo:lo + ntok], tv[:, 0:ntok], hgB[:, d, h:h + 1], hgA[:, d, h:h + 1], ALU.mult, ALU.add),
                             reads=[tb, b_const], writes=[bBc[d]])
                    elif slot == 3:
                        S.op("act", lambda e: e.copy(Vf[:, lo:lo + ntok], pv[:, 0:ntok]), reads=[pb], writes=[bV])
                    else:
                        a = max(lo, T_CTX)
                        e_ = min(lo + ntok, T_CTX + T_OWN)
                        if a < e_:
                            S.op("act", lambda e: e.activation(Z[:, a - T_CTX:e_ - T_CTX], pv[:, a - lo:e_ - lo], AF.Silu), reads=[pb], writes=[bZ])

                inproj(range(5), wt, wtb, rXB, evac, nblk_of={0: NB_OWN, 1: NB_OWN, 4: NB_OWN})
                for t in range(NT):
                    tv, tb = rTv.next()
                    S.op("pe", lambda e: e.transpose(tv[:, 0:128], Vf[:, t * 128:(t + 1) * 128], ident[:]), reads=[bV, b_const], writes=[tb])
                    S.op("act", lambda e: e.copy(Vt[:, t, :], tv[:, 0:128]), reads=[tb], writes=[bVt])

                for d in range(2):
                    TD = T_FWD if d == 0 else T_ALL
                    S.op("act", lambda e: e.activation(LFt[:, 0:TD], Bc[d][:, 0:TD], AF.Ln), reads=[bBc[d], bLF], writes=[bLF])
                    S.op("dve", lambda e: e.tensor_tensor_scan(Bc[d][:, 0:TD], LFt[:, 0:TD], LFt[:, 0:TD], 0.0, ALU.add, ALU.bypass), reads=[bLF, bBc[d]], writes=[bBc[d]])
                    ntd = (T_FWD // 128) if d == 0 else NT
                    B3 = Bc[d][:, 0:ntd * 128].rearrange("p (t j) -> p t j", j=128)
                    L3 = LFt[:, 0:ntd * 128].rearrange("p (t j) -> p t j", j=128)
                    S.op("dve", lambda e: e.tensor_tensor(small[:, 0:ntd, 3], B3[:, :, 127], B3[:, :, 0], ALU.subtract), reads=[bBc[d]] + bSmT, writes=bSmT)
                    S.op("dve", lambda e: e.tensor_tensor(small[:, 0:ntd, 3], small[:, 0:ntd, 3], L3[:, :, 0], ALU.add), reads=[bLF] + bSmT, writes=bSmT)
                    S.op("act", lambda e: e.activation(small[:, 0:ntd, 0], small[:, 0:ntd, 3], AF.Exp), reads=bSmT, writes=bSmT)
                    if d == 1:
                        S.op("dve", lambda e: e.tensor_tensor(Bc[1][:], Bc[1][:], LFt[:], ALU.subtract), reads=[bLF, bBc[1]], writes=[bBc[1]])
                    S.op("pool", lambda e: e.memset(St[:], 0.0), reads=[bS], writes=[bS])
                    mask = maskf if d == 0 else maskb
                    order = list(range(OWN1)) if d == 0 else ([1, 0] + list(range(NT - 1, OWN1 - 1, -1)) + list(range(OWN1 - 1, OWN0 - 1, -1)))
                    recs = {}

                    def m0(t, d=d):
                        sl = slice(t * 128, (t + 1) * 128)
                        e0, e1 = t * 128, t * 128 + 127
                        lat = OWN0 <= t < OWN1
                        rec = recs[t] = {"lat": lat, "sl": sl}
                        xav, xab = rXa.next()
                        if d == 0:
                            S.op("dve", lambda e: e.tensor_scalar(xav[:], Bc[0][:, sl], -1.0, Bc[0][:, e1:e1 + 1], ALU.mult, ALU.add), reads=[bBc[0]], writes=[xab])
                        else:
                            S.op("dve", lambda e: e.tensor_scalar(xav[:], Bc[1][:, sl], Bc[1][:, e0:e0 + 1], None, ALU.subtract), reads=[bBc[1]], writes=[xab])
                        rec.update(xav=xav, xab=xab)
                        if lat:
                            S.op("dve", lambda e: e.tensor_scalar(small[:, t, 4:5], xav[:, 64:65], -1.0, None, ALU.mult), reads=[bSmT[t], xab], writes=[bSmT[t]])
                            S.op("dve", lambda e: e.tensor_tensor(small[:, t, 2:3], small[:, t, 3:4], xav[:, 64:65], ALU.subtract), reads=[bSmT[t], xab], writes=[bSmT[t]])

                    def m1(t):
                        rec = recs[t]
                        ev, eb = rE1.next()
                        S.op("act", lambda e: e.activation(ev[:], rec["xav"][:], AF.Exp), reads=[rec["xab"]], writes=[eb])
                        rec.update(e1=ev, e1b=eb)
                        if rec["lat"]:
                            S.op("act", lambda e: e.activation(small[:, t, 1:2], small[:, t, 2:3], AF.Exp), reads=[bSmT[t]], writes=[bSmT[t]])
                            ev2, eb2 = rE2.next()
                            S.op("act", lambda e: e.activation(ev2[:], rec["xav"][:], AF.Exp, bias=rec["xav"][:, 64:65], scale=-1.0), reads=[rec["xab"]], writes=[eb2])
                            ev3, eb3 = rE3.next()
                            S.op("act", lambda e: e.activation(ev3[:], rec["xav"][:], AF.Exp, bias=small[:, t, 4:5], scale=1.0), reads=[rec["xab"], bSmT[t]], writes=[eb3])
                            rec.update(e2=ev2, e2b=eb2, e3=ev3, e3b=eb3)

                    def m2(t, d=d):
                        rec = recs[t]
                        sl = rec["sl"]
                        kdv, kdb = rKD.next()
                        S.op("dve", lambda e: e.tensor_tensor(kdv[:], Kd[d][:, sl], rec["e1"][:], ALU.mult), reads=[rec["e1b"], bK[d]], writes=[kdb])
                        rec.update(kdv=kdv, kdb=kdb)
                        if rec["lat"]:
                            qtv, qtb = rQt.next()
                            S.op("dve", lambda e: e.tensor_tensor(qtv[:], Q[:, sl], rec["e2"][:], ALU.mult), reads=[rec["e2b"], bQ], writes=[qtb])
                            ktv, ktb = rKt.next()
                            S.op("dve", lambda e: e.tensor_tensor(ktv[:], Kd[d][:, sl], rec["e3"][:], ALU.mult), reads=[rec["e3b"], bK[d]], writes=[ktb])
                            rec.update(qtv=qtv, qtb=qtb, ktv=ktv, ktb=ktb)

                    def m3(t):
                        rec = recs[t]
                        tv, tb = rTk.next()
                        S.op("pe", lambda e: e.transpose(tv[:, 0:128], rec["kdv"][:], ident[:]), reads=[rec["kdb"], b_const], writes=[tb])
                        rec.update(tv=tv, tb=tb)
                        if rec["lat"]:
                            sv, sbf = rS4.next()
                            S.op("pe", lambda e: e.matmul(sv, rec["ktv"][:], rec["qtv"][:], start=True, stop=True), reads=[rec["ktb"], rec["qtb"]], writes=[sbf])
                            rec.update(sv=sv, sbf=sbf)

                    def m4(t, mask=mask):
                        rec = recs[t]
                        kv, kb = rKdt.next()
                        S.op("act", lambda e: e.copy(kv[:], rec["tv"][:, 0:128]), reads=[rec["tb"]], writes=[kb])
                        rec.update(kv=kv, kb=kb)
                        if rec["lat"]:
                            pv, pbf = rP.next()
                            S.op("dve", lambda e: e.tensor_tensor(pv[:], rec["sv"], mask[:], ALU.mult), reads=[rec["sbf"], b_const], writes=[pbf])
                            rec.update(pv=pv, pbf=pbf)

                    def m5(t):
                        rec = recs[t]
                        dv, dbf = rD8.next()
                        S.op("pe", lambda e: e.matmul(dv, rec["kv"][:], Vt[:, t, :], start=True, stop=True), reads=[rec["kb"], bVt], writes=[dbf])
                        rec.update(dv=dv, dbf=dbf)

                    def m6(t):
                        rec = recs[t]
                        if rec["lat"]:
                            srv, srb = rSr.next()
                            S.op("act", lambda e: e.activation(srv[:], St[:], AF.Identity, scale=small[:, t, 1:2]), reads=[bS, bSmT[t]], writes=[srb])
                            rec.update(srv=srv, srb=srb)
                        S.op("dve", lambda e: e.scalar_tensor_tensor(St[:], St[:], small[:, t, 0:1], rec["dv"], ALU.mult, ALU.add),
                             reads=[bS, bSmT[t], rec["dbf"]], writes=[bS])

                    def m7(t, d=d):
                        rec = recs[t]
                        if not rec["lat"]:
                            return
                        ov, obf = rO8.next()
                        S.op("pe", lambda e: e.matmul(ov, rec["pv"][:], Vt[:, t, :], start=True, stop=False), reads=[rec["pbf"], bVt], writes=[obf])
                        S.op("pe", lambda e: e.matmul(ov, rec["qtv"][:], rec["srv"][:], start=False, stop=(d == 0)), reads=[rec["qtb"], rec["srb"]], writes=[obf])
                        if d == 1:
                            S.op("pe", lambda e: e.matmul(ov, ident[:], Of[:, t - NT_CTX, :], start=False, stop=True), reads=[b_const, bOf], writes=[obf])
                        rec.update(ov=ov, obf=obf)

                    def m8(t, d=d):
                        rec = recs[t]
                        if not rec["lat"]:
                            return
                        tl = t - NT_CTX
                        if d == 0:
                            S.op("act", lambda e: e.copy(Of[:, tl, :], rec["ov"]), reads=[rec["obf"]], writes=[bOf])
                            return
                        osv, osb = rOs.next()
                        S.op("act", lambda e: e.copy(osv[:], rec["ov"]), reads=[rec["obf"]], writes=[osb])
                        fv, fb = rFin.next()
                        rec.update(osv=osv, osb=osb, fv=fv, fb=fb)

                    def fin_stage(k):
                        def f(t, d=d, h=h):
                            rec = recs[t]
                            if not rec["lat"] or d == 0:
                                return
                            tl = t - NT_CTX
                            osv, osb, fv, fb = rec["osv"], rec["osb"], rec["fv"], rec["fb"]
                            if k == 0:
                                S.op("act", lambda e: e.activation(junk[:], osv[:], AF.Square, accum_out=fv[:, 0:1]), reads=[osb, bJ], writes=[fb, bJ])
                            elif k == 1:
                                S.op("act", lambda e: e.activation(fv[:, 1:2], fv[:, 0:1], AF.Ln, bias=NORM_EPS, scale=1.0 / 128.0), reads=[fb], writes=[fb])
                            elif k == 2:
                                S.op("act", lambda e: e.activation(fv[:, 2:3], fv[:, 1:2], AF.Exp, scale=-0.5), reads=[fb], writes=[fb])
                            elif k == 3:
                                onv, onbf = rOn.next()
                                S.op("dve", lambda e: e.tensor_scalar(onv[:], osv[:], fv[:, 2:3], None, ALU.mult), reads=[osb, fb], writes=[onbf])
                                rec.update(onv=onv, onbf=onbf)
                            elif k == 4:
                                tv, tb = rTf.next()
                                S.op("pe", lambda e: e.transpose(tv[:, 0:128], rec["onv"][:], ident[:]), reads=[rec["onbf"], b_const], writes=[tb])
                                rec.update(tv2=tv, tb2=tb)
                            else:
                                yv, ybf = rYt.next()
                                S.op("dve", lambda e: e.scalar_tensor_tensor(yv[:], rec["tv2"][:, 0:128], hgn[:, h:h + 1], Z[:, tl * 128:(tl + 1) * 128], ALU.mult, ALU.mult),
                                     reads=[rec["tb2"], b_const, bZ], writes=[ybf])
                                S.dma("sp", f"yo{tl % 3}", Y[tl][:, h, :], yv[:], reads=[ybf], writes=[b_y])
                        return f

                    stages = [m0, m1, m2, m3, m4, m5, m6, m7, m8] + [fin_stage(k) for k in range(6)]
                    nst = len(stages)
                    for i in range(len(order) + nst - 1):
                        for j in range(nst - 1, -1, -1):
                            if 0 <= i - j < len(order):
                                stages[j](order[i - j])
                        if i - (nst - 1) >= 0:
                            recs.pop(order[i - (nst - 1)], None)

        S.barrier()
        with ExitStack() as s0:
            wt = sb("mw", [128, 5, 16, 128], BF16, s0)
            wtb = S.buf("mw")
            xblk = [sb(f"mx{i}", [128, 4, 16, 128], BF16, s0) for i in range(2)]
            rXB = Rot(S, xblk, "mx")
            GA = sb("mGA", [48, T_ALL], F32, s0)
            GB = sb("mGB", [16, T_ALL], F32, s0)
            gsm = sb("mgsm", [16, NT, 4], F32, s0)
            gb48 = sb("mgb48", [48, 1], F32, s0)
            selb = sb("mselb", [16, 8, 128], F32, s0)
            selc = sb("mselc", [48, 48, 2], F32, s0)
            gcol = sb("mgcol", [128, NT, 2], F32, s0)
            etb = sb("metb", [128, NT], F32, s0)
            Qc = sb("mQc", [128, 2, T_FWD], BF16, s0)
            Kc = sb("mKc", [128, 2, T_ALL], BF16, s0)
            Vt = sb("mVt", [128, NT, 256], BF16, s0)
            Gg = sb("mGg", [128, 2, T_OWN], BF16, s0)
            Hf = sb("mHf", [128, NT_OWN, 256], BF16, s0)
            pre = sb("mpre", [128, 258 + 66 * 66], BF16, s0)
            dg = sb("mdg", [128, 9, 128], BF16, s0)
            pre2 = sb("mpre2", [128, 258 + 66 * 66], BF16, s0)
            rVtmp = Rot(S, [sb(f"mvtmp{i}", [128, 512], BF16, s0) for i in range(2)], "mvtmp")
            wg = rVtmp.views[0][:, 0:256].rearrange("p (a b) -> p a b", b=16)
            wg48 = dg[:].rearrange("p a b -> p (a b)")[:, 0:768].rearrange("p (a b) -> p a b", b=48)
            Ct = [sb(f"mC{c}", [128, 257], F32, s0) for c in range(2)]
            mk = lambda nm, shp, dt, n=2: Rot(S, [sb(f"{nm}{i}", shp, dt, s0) for i in range(n)], nm)
            LAG = 2
            rCd = [mk(f"mCd{c}_", [128, 257], BF16) for c in range(2)]
            rVe = mk("mVe", [128, 257], BF16, 7)
            rP = mk("mP", [128, 128], BF16, 3)
            rKt = mk("mkt", [128, 256], BF16, 5)
            rR4 = Rot(S, [banks[0][:, 0:128]], "mR", bufs=[bbuf[0]])
            rS4 = Rot(S, [banks[1][:, 0:128]], "mS", bufs=[bbuf[1]])
            rDc = [Rot(S, [banks[2 + c]], f"mDc{c}", bufs=[bbuf[2 + c]]) for c in range(2)]
            rO2 = Rot(S, [banks[4], banks[5]], "mO2", bufs=[bbuf[4], bbuf[5]])
            rG1 = rA
            rTk = Rot(S, [TBs[0][:, 0:256]], "mTk", bufs=[tbuf[0]])
            rTf = Rot(S, [TBs[1][:, 0:256]], "mTf", bufs=[tbuf[1]])
            rHn = mk("mhn", [128, 256], BF16, 3)
            rYt = mk("myt", [128, 2, 128], BF16, 3)
            rFin = mk("mfin", [128, 16], F32, 8)
            rOt = mk("mot", [128, 512], F32, 1)
            rHs = mk("mhs", [128, 256], BF16, 4)
            rRb = mk("mRb", [128, 128], F32, 3)
            rQt = mk("mQt", [128, 2, 128], BF16, 5)
            mlnh = sb("mlnh", [128, 8], F32, s0)
            bGA, bGB, bgsm, bsel, bgcol, betb, bQK, bVt, bGg, bHf, bpre, bdg, bmlnh, bpre2 = (S.buf(n) for n in
                ("mGA", "mGB", "mgsm", "msel", "mgcol", "metb", "mQK", "mVt", "mGg", "mHf", "mpre", "mdg", "mlnh", "mpre2"))
            bC = S.bufs(2, "mC")
            pres = [(Hf[:].rearrange("p a b -> p (a b)"), bHf), (Gg[:].rearrange("p a b -> p (a b)"), bGg), (pre[:], bpre), (pre2[:], bpre2)]
            QPRE = 258 + 66 * 38

            S.op("dve", lambda e: e.tensor_scalar(mlnh[:], mln[:], 0.5, None, ALU.mult), reads=[b_const], writes=[bmlnh])
            S.op("pool", lambda e: e.memset(pre[:], 0.0), writes=[bpre])
            S.op("pool", lambda e: e.memset(pre2[:], 0.0), writes=[bpre2])
            S.op("pool", lambda e: e.memset(selb[:], 1.0), writes=[bsel])
            S.op("pool", lambda e: e.affine_select(selb[:], selb[:], [[1, 8], [0, 128]], ALU.is_equal, 0.0, base=8, channel_multiplier=-1), reads=[bsel], writes=[bsel])
            S.op("pool", lambda e: e.memset(selc[:], 1.0), reads=[bsel], writes=[bsel])
            S.op("pool", lambda e: e.affine_select(selc[:], selc[:], [[1, 48], [0, 2]], ALU.is_equal, 0.0, base=0, channel_multiplier=-1), reads=[bsel], writes=[bsel])
            S.op("pool", lambda e: e.memset(gb48[:], 0.0), reads=[bsel], writes=[bsel])
            S.dma("sp", "c4", gb48[0:16, :], gateb, reads=[bsel], writes=[bsel])
            S.dma("sp", "c4", gb48[32:48, :], gateb, reads=[bsel], writes=[bsel])

            S.dma("pool", "mwg", wg, wgate, writes=[wtb])
            S.op("pool", lambda e: e.memset(wg48, 0.0), reads=[wtb], writes=[wtb])
            S.op("dve", lambda e: e.tensor_copy(wg48[:, :, 0:16], wg), reads=[wtb], writes=[wtb])
            S.op("dve", lambda e: e.tensor_copy(wg48[:, :, 32:48], wg), reads=[wtb], writes=[wtb])
            for blk in range(NBLK):
                lo = blk * 512
                ntok = min(512, T_ALL - lo)
                xv, xb_ = rXB.next()
                S.dma("sp", f"xblk{blk % 2}", xv[:, 0:ntok // 128, :, :], XM[blk][:, 0:ntok // 128, :, :], reads=[b_xm], writes=[xb_])
                pv, pb = rA.next()
                for c in range(16):
                    S.op("pe", lambda e: e.matmul(pv[0:48, 0:ntok], wg48[:, c, :], xv[:, 0:ntok // 128, c, :], start=(c == 0), stop=(c == 15)),
                         reads=[wtb, xb_], writes=[pb])
                S.op("act", lambda e: e.activation(GA[:, lo:lo + ntok], pv[0:48, 0:ntok], AF.Identity, bias=gb48[:, 0:1], scale=1.0), reads=[pb, bsel], writes=[bGA])
            S.op("act", lambda e: e.activation(GA[0:16, :], GA[0:16, :], AF.Exp, scale=-1.0), reads=[bGA], writes=[bGA])
            S.op("act", lambda e: e.activation(GA[0:16, :], GA[0:16, :], AF.Ln, bias=1.0, scale=1.0), reads=[bGA], writes=[bGA])
            S.op("dve", lambda e: e.tensor_scalar(GA[0:16, :], GA[0:16, :], -1.0, None, ALU.mult), reads=[bGA], writes=[bGA])
            S.op("dve", lambda e: e.tensor_tensor_scan(GB[:], GA[0:16, :], GA[0:16, :], 0.0, ALU.add, ALU.bypass), reads=[bGA], writes=[bGB])
            G3 = GB[:].rearrange("p (t j) -> p t j", j=128)
            A3 = GA[0:16, :].rearrange("p (t j) -> p t j", j=128)
            S.op("dve", lambda e: e.tensor_tensor(gsm[:, :, 0], G3[:, :, 127], G3[:, :, 0], ALU.subtract), reads=[bGB, bgsm], writes=[bgsm])
            S.op("dve", lambda e: e.tensor_tensor(gsm[:, :, 0], gsm[:, :, 0], A3[:, :, 0], ALU.add), reads=[bGA, bgsm], writes=[bgsm])
            S.op("dve", lambda e: e.tensor_copy(gsm[:, :, 2], G3[:, :, 127]), reads=[bGB, bgsm], writes=[bgsm])
            S.op("dve", lambda e: e.tensor_tensor(GA[0:16, :], GB[:], GA[0:16, :], ALU.subtract), reads=[bGA, bGB], writes=[bGA])
            S.op("dve", lambda e: e.tensor_copy(gsm[:, :, 1], A3[:, :, 0]), reads=[bGA, bgsm], writes=[bgsm])
            for t in range(NT):
                sl = slice(t * 128, (t + 1) * 128)
                S.op("dve", lambda e: e.tensor_scalar(GA[0:16, sl], GA[0:16, sl], gsm[:, t, 1:2], None, ALU.subtract), reads=[bGA, bgsm], writes=[bGA])
                S.op("dve", lambda e: e.tensor_scalar(GB[:, sl], GB[:, sl], -1.0, gsm[:, t, 2:3], ALU.mult, ALU.add), reads=[bGB, bgsm], writes=[bGB])

            S.barrier()

            def evac_v(blk, ntok, pv, pb, half):
                nt_ = ntok // 128
                vtv, vtb = rVtmp.next()
                S.op("act", lambda e: e.copy(vtv[:, 0:ntok], pv[:, 0:ntok]), reads=[pb], writes=[vtb])
                tv, tb = rT.next()
                for j in range(nt_):
                    S.op("pe", lambda e: e.transpose(tv[:, j * 128:(j + 1) * 128], vtv[:, j * 128:(j + 1) * 128], ident[:]), reads=[vtb, b_const], writes=[tb])
                S.op("act", lambda e: e.copy(Vt[:, blk * 4:blk * 4 + nt_, half * 128:(half + 1) * 128], tv[:, 0:ntok].rearrange("p (a b) -> p a b", b=128)),
                     reads=[tb], writes=[bVt])

            for h in range(ML_H):
                for half in range(2):
                    for sl_ in range(5):
                        S.dma("pool", "mw", wt[:, sl_, :, :], win[40 + h * 10 + half * 5 + sl_], writes=[wtb], max_dma_last_dim=4096)
                    if half == 0:
                        for k in (0, 1):
                            S.op("pool", lambda e: e.memset(pres[k][0][:, 0:QPRE], 0.0), reads=[pres[k][1]], writes=[pres[k][1]])

                        def evac(slot, blk, ntok, pv, pb, h=h):
                            lo = blk * 512
                            if slot == 4:
                                evac_v(blk, ntok, pv, pb, 0)
                                return
                            prv, prb = pres[slot]
                            if lo < T_CTX:
                                S.op("act", lambda e: e.copy(prv[:, 1:1 + T_CTX], pv[:, 0:T_CTX]), reads=[pb], writes=[prb])
                                a0, r0, nr = T_CTX, 0, 4
                            else:
                                a0, r0, nr = 0, (lo - T_CTX) // 64, ntok // 64
                            gsz = (QPRE - 258) if slot < 2 else 66 * 66
                            dst = prv[:, 258:258 + gsz].rearrange("p (r c) -> p r c", c=66)[:, 1 + r0:1 + r0 + nr, 1:65]
                            S.op("act", lambda e: e.copy(dst, pv[:, a0:a0 + nr * 64].rearrange("p (r c) -> p r c", c=64)), reads=[pb], writes=[prb])

                        inproj(range(5), wt, wtb, rXB, evac, nblk_of={0: NB_OWN, 1: NB_OWN})
                        for slot in range(4):
                            prv, prb = pres[slot]
                            ch = (2 * h + slot) if slot < 2 else (8 + 2 * h + slot - 2)
                            for tap in range(9):
                                S.op("dve", lambda e: e.tensor_scalar(dg[:, tap, :], ident[:], cw[:, ch, tap:tap + 1], None, ALU.mult), reads=[b_const, bdg], writes=[bdg])
                            pv, pb = rA6g.next()
                            for j in range(3):
                                S.op("pe", lambda e: e.matmul(pv[:, 0:T_CTX], dg[:, 3 + j, :], prv[:, j:j + T_CTX], start=(j == 0), stop=(j == 2)), reads=[bdg, prb], writes=[pb])
                            S.op("act", lambda e: e.activation((Qc[:, slot, 0:T_CTX] if slot < 2 else Kc[:, slot - 2, 0:T_CTX]), pv[:, 0:T_CTX], AF.Silu, bias=cb[:, ch:ch + 1], scale=1.0), reads=[pb, b_const], writes=[bQK])
                            gsz = (QPRE - 258) if slot < 2 else 66 * 66
                            grid = prv[:, 258:258 + gsz].rearrange("p (r c) -> p r c", c=66)
                            for rb8 in range(4 if slot < 2 else 8):
                                pv, pb = rA6g.next()
                                for tap in range(9):
                                    di, dj = tap // 3, tap % 3
                                    S.op("pe", lambda e: e.matmul(pv[:], dg[:, tap, :], grid[:, di + 8 * rb8:di + 8 * rb8 + 8, dj:dj + 64], start=(tap == 0), stop=(tap == 8)),
                                         reads=[bdg, prb], writes=[pb])
                                S.op("act", lambda e: e.activation((Qc[:, slot, T_CTX + rb8 * 512:T_CTX + (rb8 + 1) * 512] if slot < 2 else Kc[:, slot - 2, T_CTX + rb8 * 512:T_CTX + (rb8 + 1) * 512]), pv[:], AF.Silu, bias=cb[:, ch:ch + 1], scale=1.0),
                                     reads=[pb, b_const], writes=[bQK])
                    else:
                        def evac(slot, blk, ntok, pv, pb, h=h):
                            lo = blk * 512
                            if slot == 0:
                                evac_v(blk, ntok, pv, pb, 1)
                                return
                            a = max(lo, T_CTX)
                            e_ = min(lo + ntok, T_CTX + T_OWN)
                            if a >= e_:
                                return
                            n = e_ - a
                            c = (slot - 1) % 2
                            dst = Gg[:, c, a - T_CTX:a - T_CTX + n]
                            tv, tb = rOt.next()
                            if slot in (1, 2):
                                S.op("act", lambda e: e.activation(tv[:, 0:n], pv[:, a - lo:e_ - lo], AF.Tanh, scale=0.5), reads=[pb], writes=[tb])
                                S.op("dve", lambda e: e.tensor_scalar(dst, tv[:, 0:n], 1.0, None, ALU.add), reads=[tb, bGg], writes=[bGg])
                            else:
                                S.op("act", lambda e: e.activation(tv[:, 0:n], pv[:, a - lo:e_ - lo], AF.Silu), reads=[pb], writes=[tb])
                                S.op("dve", lambda e: e.tensor_tensor(dst, dst, tv[:, 0:n], ALU.mult), reads=[tb, bGg], writes=[bGg])

                        inproj(range(5), wt, wtb, rXB, evac, nblk_of={1: NB_OWN, 2: NB_OWN, 3: NB_OWN, 4: NB_OWN})

                for d in range(2):
                    frow = 8 + d * 4 + h
                    irow = 32 + d * 4 + h
                    Xarr, bX = (GB, bGB) if d == 0 else (GA, bGA)
                    pv, pb = rG1.next()
                    S.op("pe", lambda e: e.matmul(pv[:, 0:NT], selb[:, frow - 8, :], gsm[:, :, 0], start=True, stop=True), reads=[bsel, bgsm], writes=[pb])
                    S.op("act", lambda e: e.activation(etb[:], pv[:, 0:NT], AF.Exp), reads=[pb], writes=[betb])
                    pv, pb = rG1.next()
                    for t in range(NT):
                        sl = slice(t * 128, (t + 1) * 128)
                        if d == 0:
                            S.op("pe", lambda e: e.matmul(pv[:, 2 * t:2 * t + 2], GB[:, sl], selc[0:16, frow, :], start=True, stop=False), reads=[bGB, bsel], writes=[pb])
                        else:
                            S.op("pe", lambda e: e.matmul(pv[:, 2 * t:2 * t + 2], GA[:, sl], selc[:, frow, :], start=True, stop=False), reads=[bGA, bsel], writes=[pb])
                        S.op("pe", lambda e: e.matmul(pv[:, 2 * t:2 * t + 2], GA[:, sl], selc[:, irow, :], start=False, stop=True), reads=[bGA, bsel], writes=[pb])
                    S.op("act", lambda e: e.activation(gcol[:].rearrange("p a b -> p (a b)"), pv[:, 0:2 * NT], AF.Exp), reads=[pb], writes=[bgcol])

                    for c in range(2):
                        S.op("pool", lambda e: e.memset(Ct[c][:], 0.0), reads=[bC[c]], writes=[bC[c]])
                    mask = maskf if d == 0 else maskb
                    order = list(range(OWN1)) if d == 0 else ([1, 0] + list(range(NT - 1, OWN1 - 1, -1)) + list(range(OWN1 - 1, OWN0 - 1, -1)))
                    recs = {}

                    def m0(t, mask=mask):
                        sl = slice(t * 128, (t + 1) * 128)
                        lat = OWN0 <= t < OWN1
                        rec = recs[t] = {"lat": lat, "sl": sl}
                        vev, veb = rVe.next()
                        S.op("act", lambda e: e.activation(vev[:, 0:256], Vt[:, t, :], AF.Identity, scale=gcol[:, t, 0:1]), reads=[bVt, bgcol], writes=[veb])
                        S.op("act", lambda e: e.copy(vev[:, 256:257], gcol[:, t, 0:1]), reads=[bgcol, veb], writes=[veb])
                        tv, tb = rTk.next()
                        for c in range(2):
                            S.op("pe", lambda e: e.transpose(tv[:, c * 128:(c + 1) * 128], Kc[:, c, sl], ident[:]), reads=[bQK, b_const], writes=[tb])
                        rec.update(vev=vev, veb=veb, tv=tv, tb=tb)
                        if lat:
                            rv, rb_ = rR4.next()
                            S.op("pe", lambda e: e.matmul(rv, selb[:, frow - 8, :], Xarr[0:16, sl], start=True, stop=True), reads=[bsel, bX], writes=[rb_])
                            rec.update(rv=rv, rb_=rb_)

                    def m1(t):
                        rec = recs[t]
                        kv, kb = rKt.next()
                        S.op("act", lambda e: e.copy(kv[:], rec["tv"][:, 0:256]), reads=[rec["tb"]], writes=[kb])
                        rec.update(kv=kv, kb=kb)
                        if rec["lat"]:
                            rbv, rbb = rRb.next()
                            S.op("act", lambda e: e.activation(rbv[:], rec["rv"], AF.Exp, bias=-math.log(16.0), scale=-1.0), reads=[rec["rb_"]], writes=[rbb])
                            rec.update(rbv=rbv, rbb=rbb)

                    def m2(t):
                        rec = recs[t]
                        if not rec["lat"]:
                            return
                        qtv, qtb = rQt.next()
                        for c in range(2):
                            S.op("dve", lambda e: e.tensor_tensor(qtv[:, c, :], Qc[:, c, rec["sl"]], rec["rbv"][:], ALU.mult), reads=[bQK, rec["rbb"]], writes=[qtb])
                        rec.update(qtv=qtv, qtb=qtb)

                    def m3(t):
                        rec = recs[t]
                        if not rec["lat"]:
                            return
                        sv, sbf = rS4.next()
                        for c in range(2):
                            S.op("pe", lambda e: e.matmul(sv, Kc[:, c, rec["sl"]], rec["qtv"][:, c, :], start=(c == 0), stop=(c == 1)), reads=[bQK, rec["qtb"]], writes=[sbf])
                        rec.update(sv=sv, sbf=sbf)

                    def m4(t, mask=mask):
                        rec = recs[t]
                        if not rec["lat"]:
                            return
                        ppv, pbf = rP.next()
                        S.op("dve", lambda e: e.tensor_tensor(ppv[:], rec["sv"], mask[:], ALU.mult), reads=[rec["sbf"], b_const], writes=[pbf])
                        rec.update(ppv=ppv, pbf=pbf)

                    def m5(t):
                        rec = recs[t]
                        if rec["lat"]:
                            cdv = []
                            for c in range(2):
                                v_, b_ = rCd[c].next()
                                S.op("act", lambda e: e.activation(v_[:], Ct[c][:], AF.Identity, scale=etb[:, t:t + 1]), reads=[bC[c], betb], writes=[b_])
                                cdv.append((v_, b_))
                            rec.update(cdv=cdv)
                        dcs = []
                        for c in range(2):
                            dv, dbf = rDc[c].next()
                            S.op("pe", lambda e: e.matmul(dv[:, 0:257], rec["kv"][:, c * 128:(c + 1) * 128], rec["vev"][:], start=True, stop=True), reads=[rec["kb"], rec["veb"]], writes=[dbf])
                            dcs.append((dv, dbf))
                        rec.update(dcs=dcs)

                    def m6(t):
                        rec = recs[t]
                        for c in range(2):
                            dv, dbf = rec["dcs"][c]
                            S.op("dve", lambda e: e.scalar_tensor_tensor(Ct[c][:], Ct[c][:], etb[:, t:t + 1], dv[:, 0:257], ALU.mult, ALU.add),
                                 reads=[bC[c], betb, dbf], writes=[bC[c]])
                        if rec["lat"]:
                            ov, obf = rO2.next()
                            S.op("pe", lambda e: e.matmul(ov[:, 0:257], rec["ppv"][:], rec["vev"][:], start=True, stop=False), reads=[rec["pbf"], rec["veb"]], writes=[obf])
                            for c in range(2):
                                S.op("pe", lambda e: e.matmul(ov[:, 0:257], rec["qtv"][:, c, :], rec["cdv"][c][0][:], start=False, stop=(c == 1)),
                                     reads=[rec["qtb"], rec["cdv"][c][1]], writes=[obf])
                            rec.update(ov=ov, obf=obf)

                    def m7(t):
                        rec = recs[t]
                        if not rec["lat"]:
                            return
                        fv, fb = rFin.next()
                        S.op("act", lambda e: e.activation(fv[:, 12:13], rec["ov"][:, 256:257], AF.Abs), reads=[rec["obf"]], writes=[fb])
                        rec.update(fv=fv, fb=fb)

                    def m8(t, d=d):
                        rec = recs[t]
                        if not rec["lat"]:
                            return
                        tl = t - NT_CTX
                        fv, fb, ov, obf = rec["fv"], rec["fb"], rec["ov"], rec["obf"]
                        S.op("dve", lambda e: e.tensor_scalar(fv[:, 0:1], fv[:, 12:13], 1.0, None, ALU.max), reads=[fb], writes=[fb])
                        S.op("dve", lambda e: e.reciprocal(fv[:, 1:2], fv[:, 0:1]), reads=[fb], writes=[fb])
                        if d == 0:
                            S.op("dve", lambda e: e.tensor_scalar(Hf[:, tl, :], ov[:, 0:256], fv[:, 1:2], None, ALU.mult), reads=[obf, fb], writes=[bHf])
                            return
                        hv, hb = rHs.next()
                        S.op("dve", lambda e: e.scalar_tensor_tensor(hv[:], ov[:, 0:256], fv[:, 1:2], Hf[:, tl, :], ALU.mult, ALU.add), reads=[obf, fb, bHf], writes=[hb])
                        rec.update(hv=hv, hb=hb)

                    def fin_stage(k):
                        def f(t, d=d, h=h):
                            rec = recs[t]
                            if not rec["lat"] or d == 0:
                                return
                            tl = t - NT_CTX
                            fv, fb, hv, hb = rec["fv"], rec["fb"], rec["hv"], rec["hb"]
                            if k == 0:
                                S.op("dve", lambda e: e.bn_stats(fv[:, 2:8], hv[:]), reads=[hb, fb], writes=[fb])
                                S.op("dve", lambda e: e.bn_aggr(fv[:, 8:10], fv[:, 2:8]), reads=[fb], writes=[fb])
                            elif k == 1:
                                S.op("act", lambda e: e.activation(fv[:, 10:11], fv[:, 9:10], AF.Ln, bias=NORM_EPS, scale=1.0), reads=[fb], writes=[fb])
                                S.op("act", lambda e: e.activation(fv[:, 11:12], fv[:, 10:11], AF.Exp, scale=-0.5), reads=[fb], writes=[fb])
                            elif k == 2:
                                hnv, hnb = rHn.next()
                                S.op("dve", lambda e: e.tensor_scalar(hnv[:], hv[:], fv[:, 8:9], fv[:, 11:12], ALU.subtract, ALU.mult), reads=[hb, fb], writes=[hnb])
                                rec.update(hnv=hnv, hnb=hnb)
                            elif k == 3:
                                tv, tb = rTf.next()
                                for c in range(2):
                                    S.op("pe", lambda e: e.transpose(tv[:, c * 128:(c + 1) * 128], rec["hnv"][:, c * 128:(c + 1) * 128], ident[:]), reads=[rec["hnb"], b_const], writes=[tb])
                                rec.update(tv2=tv, tb2=tb)
                            else:
                                yv, ybf = rYt.next()
                                for c in range(2):
                                    S.op("dve", lambda e: e.scalar_tensor_tensor(yv[:, c, :], rec["tv2"][:, c * 128:(c + 1) * 128], mlnh[:, 2 * h + c:2 * h + c + 1],
                                                                                Gg[:, c, tl * 128:(tl + 1) * 128], ALU.mult, ALU.mult),
                                         reads=[rec["tb2"], bmlnh, bGg], writes=[ybf])
                                S.dma("sp", f"yo{tl % 3}", Y[tl][:, 8 + 2 * h:8 + 2 * h + 2, :], yv[:], reads=[ybf], writes=[b_y])
                        return f

                    stages = [m0, m1, m2, m3, m4, m5, m6, m7, m8] + [fin_stage(k) for k in range(5)]
                    nst = len(stages)
                    for i in range(len(order) + nst - 1):
                        for j in range(nst - 1, -1, -1):
                            if 0 <= i - j < len(order):
                                stages[j](order[i - j])
                        if i - (nst - 1) >= 0:
                            recs.pop(order[i - (nst - 1)], None)

        S.barrier()
        with ExitStack() as s0:
            wo = sb("wo", [128, 16, D], BF16, s0)
            lg = sb("lg", [128, D], F32, s0)
            lbb = sb("lbb", [128, D], F32, s0)
            bw = S.buf("wo")
            rA6 = Rot(S, banks, "pA6", bufs=bbuf)
            for c4 in range(4):
                S.dma("pool", "wo", wo[:, c4 * 4:(c4 + 1) * 4, :], wout[:, c4 * 4:(c4 + 1) * 4, :], writes=[bw], max_dma_last_dim=4096)
            S.dma("sp", "c6", lg[:], lng.partition_broadcast(128), writes=[bw])
            S.dma("sp", "c7", lbb[:], lnb.partition_broadcast(128), writes=[bw])
            yb = [sb(f"yb{i}", [128, 16, 128], BF16, s0) for i in range(3)]
            xr = [sb(f"xr{i}", [128, D], F32, s0) for i in range(3)]
            ob = [sb(f"ob{i}", [128, D], F32, s0) for i in range(2)]
            stt = [sb(f"ost{i}", [128, 4, 6], F32, s0) for i in range(2)]
            mv = [sb(f"omv{i}", [128, 4], F32, s0) for i in range(2)]
            rYb, rXr, rOb, rSt, rMv = Rot(S, yb, "yb"), Rot(S, xr, "xr"), Rot(S, ob, "ob"), Rot(S, stt, "ost"), Rot(S, mv, "omv")
            ld_q = []

            def load3(tl):
                yv, ybf = rYb.next()
                S.dma("sp", f"yi{tl % 3}", yv[:], Y[tl], reads=[b_y], writes=[ybf])
                xv, xbf = rXr.next()
                S.dma("sp", f"xr{tl % 3}", xv[:], xt[T_CTX + tl * 128:T_CTX + (tl + 1) * 128, :], writes=[xbf])
                ld_q.append((yv, ybf, xv, xbf))

            NT3 = NT_OWN
            load3(0)
            load3(1)
            for tl in range(NT3):
                if tl + 2 < NT3:
                    load3(tl + 2)
                yv, ybf, xv, xbf = ld_q.pop(0)
                ov, obf = rOb.next()
                for nb in range(4):
                    pv, pb = rA6.next()
                    for c in range(16):
                        S.op("pe", lambda e: e.matmul(pv[:], yv[:, c, :], wo[:, c, nb * 512:(nb + 1) * 512], start=(c == 0), stop=(c == 15)),
                             reads=[ybf, bw], writes=[pb])
                    S.op("dve", lambda e: e.tensor_tensor(ov[:, nb * 512:(nb + 1) * 512], pv[:], gatex[:, nb * 512:(nb + 1) * 512], ALU.mult),
                         reads=[pb, b_mod], writes=[obf])
                S.op("dve", lambda e: e.scalar_tensor_tensor(ov[:], xv[:], ALPHA, ov[:], ALU.mult, ALU.add), reads=[xbf, obf], writes=[obf])
                sv, sbf = rSt.next()
                for i in range(4):
                    S.op("dve", lambda e: e.bn_stats(sv[:, i, :], ov[:, i * 512:(i + 1) * 512]), reads=[obf], writes=[sbf])
                mvv, mvb = rMv.next()
                S.op("dve", lambda e: e.bn_aggr(mvv[:, 0:2], sv[:].rearrange("p a b -> p (a b)")), reads=[sbf], writes=[mvb])
                S.op("act", lambda e: e.activation(mvv[:, 2:3], mvv[:, 1:2], AF.Ln, bias=LN_EPS, scale=1.0), reads=[mvb], writes=[mvb])
                S.op("act", lambda e: e.activation(mvv[:, 3:4], mvv[:, 2:3], AF.Exp, scale=-0.5), reads=[mvb], writes=[mvb])
                S.op("dve", lambda e: e.tensor_scalar(ov[:], ov[:], mvv[:, 0:1], mvv[:, 3:4], ALU.subtract, ALU.mult), reads=[obf, mvb], writes=[obf])
                S.op("dve", lambda e: e.tensor_tensor(ov[:], ov[:], lg[:], ALU.mult), reads=[obf, bw], writes=[obf])
                S.op("dve", lambda e: e.tensor_tensor(ov[:], ov[:], lbb[:], ALU.add), reads=[obf, bw], writes=[obf])
                S.dma("sp", f"oo{tl % 2}", out[tl * 128:(tl + 1) * 128, :], ov[:], reads=[obf], writes=[])
        S.wait_all("sp")
    return nc


_PROGRAM = None


def _layout_inputs(x, c, ctx, c_ctx, w_mod, b_mod, w_in, conv_w, conv_b, hg_lb, ml_gate_b,
                   hg_norm_w, ml_norm_w, w_out, ln_g, ln_b):
    f = lambda a: np.ascontiguousarray(np.asarray(a, dtype=np.float32))
    x, c, ctx, c_ctx = f(x), f(c), f(ctx), f(c_ctx)
    w_mod, b_mod, w_in = f(w_mod)[0], f(b_mod)[0], f(w_in)[0]
    conv_w, conv_b, hg_lb = f(conv_w)[0], f(conv_b)[0], f(hg_lb)
    gate_b, hgn, mln, w_out = f(ml_gate_b)[0], f(hg_norm_w)[0], f(ml_norm_w)[0], f(w_out)[0]
    ln_g, ln_b = f(ln_g)[0], f(ln_b)[0]
    WA = 1024
    blocks = []
    for h in range(HG_H):
        for g in range(5):
            blocks.append(np.arange(g * WA + h * 128, g * WA + (h + 1) * 128))
    base = 5 * WA
    for h in range(ML_H):
        qs = base + h * 256
        ks = base + 1024 + h * 256
        vs = base + 2048 + h * 256
        os_ = base + 3072 + h * 256
        zs = base + 4096 + h * 256
        for s in (qs, qs + 128, ks, ks + 128, vs, vs + 128, os_, os_ + 128, zs, zs + 128):
            blocks.append(np.arange(s, s + 128))
    cols = np.concatenate(blocks)
    wsel = w_in[:, cols]
    win = np.ascontiguousarray(wsel.reshape(16, 128, 80, 128).transpose(2, 1, 0, 3))
    wg = w_in[:, base + 5 * 1024:base + 5 * 1024 + 16]
    wgate = np.ascontiguousarray(wg.reshape(16, 128, 16).transpose(1, 0, 2))
    wmod = np.ascontiguousarray(w_mod.reshape(16, 128, 12, 512).transpose(2, 1, 0, 3))
    bmod = np.ascontiguousarray(b_mod.reshape(1, 6144))
    bmodfm = np.ascontiguousarray(b_mod.reshape(48, 128).T)
    convw = np.ascontiguousarray(conv_w.reshape(9, 16, 128).transpose(2, 1, 0))
    convb = np.ascontiguousarray(conv_b.reshape(16, 128).T)
    lower_in = hg_lb[:, :, :]
    hglb = np.ascontiguousarray(lower_in.reshape(2, 2, 8, 128).transpose(3, 0, 1, 2))
    gateb = np.ascontiguousarray(gate_b.reshape(16, 1))
    hgnw = np.ascontiguousarray(hgn.reshape(8, 128).T)
    mlnw = np.ascontiguousarray(mln.reshape(8, 128).T)
    wout = np.ascontiguousarray(w_out.reshape(16, 128, D).transpose(1, 0, 2))
    lng = np.ascontiguousarray(ln_g.reshape(1, D))
    lnb = np.ascontiguousarray(ln_b.reshape(1, D))
    def swap_pairs(blocks_list):
        out_ = list(blocks_list)
        for h in range(HG_H):
            out_[h * 5 + 1], out_[h * 5 + 2] = out_[h * 5 + 2], out_[h * 5 + 1]
        return out_
    win_m = [win, np.ascontiguousarray(win[swap_pairs(list(range(80)))])]
    gperm = np.array([4, 5, 6, 7, 0, 1, 2, 3, 12, 13, 14, 15, 8, 9, 10, 11])
    wgate_m = [wgate, np.ascontiguousarray(wgate[:, :, gperm])]
    gateb_m = [gateb, np.ascontiguousarray(gateb[gperm])]
    hglb_m = [hglb, np.ascontiguousarray(hglb[:, ::-1])]
    convw_m = [convw, np.ascontiguousarray(convw[:, :, ::-1])]
    maps = []
    for core in range(N_CORES):
        b, m = core // 2, core % 2
        if m == 0:
            xt = np.concatenate([ctx[b], x[b, :T_OWN], x[b, T_OWN:]], axis=0)
        else:
            xt = np.concatenate([ctx[b][::-1], x[b, T_OWN:][::-1], x[b, :T_OWN][::-1]], axis=0)
        xt = np.ascontiguousarray(xt)
        cv = np.stack([c[b], c_ctx], axis=0)
        cvec = np.ascontiguousarray(cv.reshape(2, 16, 128).transpose(2, 0, 1))
        maps.append({"xt": xt, "cvec": cvec, "wmod": wmod, "bmod": bmod, "bmodfm": bmodfm, "win": win_m[m], "wgate": wgate_m[m],
                     "convw": convw_m[m], "convb": convb, "hglb": hglb_m[m], "gateb": gateb_m[m], "hgnw": hgnw, "mlnw": mlnw,
                     "wout": wout, "lng": lng, "lnb": lnb})
    return maps


def kernel(**inputs):
    global _PROGRAM
    if _PROGRAM is None:
        _PROGRAM = build_program()
    maps = _layout_inputs(**inputs)
    res = run_bass_kernel_spmd(_PROGRAM, maps, core_ids=list(range(N_CORES)))
    full = np.empty((4, T_LAT, D), dtype=np.float32)
    for core in range(N_CORES):
        b, m = core // 2, core % 2
        o = np.asarray(res.results[core]["out"], dtype=np.float32)
        if m == 0:
            full[b, :T_OWN] = o
        else:
            full[b, T_OWN:] = o[::-1]
    return full
```

```python
import math
from contextlib import ExitStack
import numpy as np
import concourse.bass as bass
import concourse.mybir as mybir
from concourse.bass_utils import run_bass_kernel_spmd

F32 = mybir.dt.float32
BF16 = mybir.dt.bfloat16
ALU = mybir.AluOpType
AF = mybir.ActivationFunctionType

D = 2048
T_CTX = 256
T_LAT = 4096
T_ALL = T_CTX + T_LAT
NT = T_ALL // 128
NT_CTX = T_CTX // 128
NBLK = (T_ALL + 511) // 512
T_OWN = 2048
OWN0, OWN1 = NT_CTX, NT_CTX + T_OWN // 128
NT_OWN = T_OWN // 128
T_FWD = 2560
NB_OWN = 5
HG_H = 8
ML_H = 4
LN_EPS = 1e-5
NORM_EPS = 1e-6
ALPHA = 2.0 ** 0.25
N_CORES = 8


class Buf:
    __slots__ = ("name", "writer", "readers", "excl")

    def __init__(self, name):
        self.name = name
        self.writer = None
        self.readers = {}
        self.excl = False


class Sched:
    def __init__(self, nc, stack):
        self.nc = nc
        self.stack = stack
        self.eng = {"pe": nc.tensor, "act": nc.scalar, "dve": nc.vector, "pool": nc.gpsimd, "sp": nc.sync}
        self.sem, self.cnt, self.waited = {}, {}, {}
        for e in self.eng:
            self.sem[e] = stack.enter_context(nc.semaphore("sem_" + e))
            self.cnt[e] = 0
            self.waited[e] = {}
        self.dsem, self.dcnt = {}, {}
        self.nb = 0

    def buf(self, name=None):
        self.nb += 1
        return Buf(name or f"b{self.nb}")

    def bufs(self, n, name="r"):
        return [self.buf(f"{name}{i}") for i in range(n)]

    def _q(self, q):
        if q not in self.dsem:
            self.dsem[q] = self.stack.enter_context(self.nc.semaphore("dq_" + q))
            self.dcnt[q] = 0

    def _wait(self, e, dep):
        src, idx = dep
        if src == e and e == "pe":
            return
        w = self.waited[e]
        if w.get(src, 0) >= idx:
            return
        w[src] = idx
        self.eng[e].wait_ge(self.sem[src] if src in self.sem else self.dsem[src], idx)

    def _deps(self, e, reads, writes):
        for b in reads:
            if b.writer is not None:
                self._wait(e, b.writer)
            if b.excl:
                for src, idx in list(b.readers.items()):
                    if src != e:
                        self._wait(e, (src, idx))
        for b in writes:
            if b.writer is not None:
                self._wait(e, b.writer)
            for src, idx in list(b.readers.items()):
                self._wait(e, (src, idx))

    def _mark(self, me, reads, writes):
        for b in reads:
            if b.readers.get(me[0], 0) < me[1]:
                b.readers[me[0]] = me[1]
        for b in writes:
            b.writer = me
            b.readers = {}

    def op(self, e, fn, reads=(), writes=()):
        self._deps(e, reads, writes)
        inst = fn(self.eng[e])
        self.cnt[e] += 1
        inst.then_inc(self.sem[e], 1)
        self._mark((e, self.cnt[e]), reads, writes)
        return inst

    def dma(self, e, q, out, in_, reads=(), writes=(), **kw):
        self._q(q)
        self._deps(e, reads, writes)
        if self.dcnt[q] > 0:
            self._wait(e, (q, self.dcnt[q]))
        inst = self.eng[e].dma_start(out=out, in_=in_, **kw)
        self.dcnt[q] += 16
        inst.then_inc(self.dsem[q], 16)
        self._mark((q, self.dcnt[q]), reads, writes)
        return inst

    def barrier(self):
        for e in self.eng:
            self.wait_all(e)

    def wait_all(self, e):
        for src in list(self.sem):
            if self.cnt[src] > 0:
                self._wait(e, (src, self.cnt[src]))
        for q in list(self.dsem):
            if self.dcnt[q] > 0:
                self._wait(e, (q, self.dcnt[q]))


class Rot:
    def __init__(self, S, views, name, bufs=None):
        self.views = views
        self.bufs = bufs if bufs is not None else S.bufs(len(views), name)
        self.i = 0

    def next(self):
        k = self.i % len(self.views)
        self.i += 1
        return self.views[k], self.bufs[k]


def build_program(debug=False, stop_after=99):
    nc = bass.Bass("TRN2", target_bir_lowering=False)
    dt_in = lambda name, shape: nc.dram_tensor(name, shape, F32, kind="ExternalInput").ap()
    xt = dt_in("xt", [T_ALL, D])
    cvec = dt_in("cvec", [128, 2, 16])
    wmod = dt_in("wmod", [12, 128, 16, 512])
    bmod = dt_in("bmod", [1, 6144])
    bmodfm = dt_in("bmodfm", [128, 48])
    win = dt_in("win", [80, 128, 16, 128])
    wgate = dt_in("wgate", [128, 16, 16])
    convw = dt_in("convw", [128, 16, 9])
    convb = dt_in("convb", [128, 16])
    hglb = dt_in("hglb", [128, 2, 2, 8])
    gateb = dt_in("gateb", [16, 1])
    hgnw = dt_in("hgnw", [128, 8])
    mlnw = dt_in("mlnw", [128, 8])
    wout = dt_in("wout", [128, 16, D])
    lng = dt_in("lng", [1, D])
    lnb = dt_in("lnb", [1, D])
    out = nc.dram_tensor("out", [T_OWN, D], F32, kind="ExternalOutput").ap()
    XM = nc.dram_tensor("xm_scr", [NBLK, 128, 4, 16, 128], BF16, **({"kind": "ExternalOutput"} if debug else {})).ap()
    Y = nc.dram_tensor("y_scr", [NT_OWN, 128, 16, 128], BF16, **({"kind": "ExternalOutput"} if debug else {})).ap()

    with ExitStack() as st:
        S = Sched(nc, st)
        sb = lambda name, shape, dt=F32, stack=st: stack.enter_context(nc.sbuf_tensor(name, shape, dt))
        ps = lambda name, shape, dt=F32, stack=st: stack.enter_context(nc.psum_tensor(name, shape, dt))

        ident = sb("ident", [128, 128], BF16)
        maskf = sb("maskf", [128, 128], F32)
        maskb = sb("maskb", [128, 128], F32)
        gatex = sb("gatex", [128, 2048], F32)
        hgA = sb("hgA", [128, 2, 8], F32)
        hgB = sb("hgB", [128, 2, 8], F32)
        hgnBm = sb("hgnBm", [128, 2, 8], F32)
        hgn = sb("hgn", [128, 8], F32)
        mln = sb("mln", [128, 8], F32)
        cw = sb("cw", [128, 16, 9], F32)
        cb = sb("cb", [128, 16], F32)
        gb = sb("gb", [16, 1], F32)
        b_const = S.buf("const")
        b_mod = S.buf("mod")
        msf = sb("msf", [128, 32, 2], F32)

        banks = [ps(f"bk{i}", [128, 512], F32) for i in range(6)]
        TBs = [ps(f"tbk{i}", [128, 1024], BF16) for i in range(2)]
        bbuf = S.bufs(6, "bank")
        tbuf = S.bufs(2, "tbank")
        for b_ in bbuf + tbuf:
            b_.excl = True
        rA = Rot(S, [banks[0], banks[1]], "pA", bufs=[bbuf[0], bbuf[1]])
        rT = Rot(S, [TBs[0], TBs[1]], "pT", bufs=[tbuf[0], tbuf[1]])
        rA6g = Rot(S, banks, "pA6g", bufs=bbuf)

        S.op("pool", lambda e: e.memset(ident[:], 1.0), writes=[b_const])
        S.op("pool", lambda e: e.affine_select(ident[:], ident[:], [[-1, 128]], ALU.is_equal, 0.0, base=0, channel_multiplier=1),
             reads=[b_const], writes=[b_const])
        S.op("pool", lambda e: e.memset(maskf[:], 1.0), writes=[b_const])
        S.op("pool", lambda e: e.affine_select(maskf[:], maskf[:], [[1, 128]], ALU.is_ge, 0.0, base=0, channel_multiplier=-1),
             reads=[b_const], writes=[b_const])
        S.op("pool", lambda e: e.memset(maskb[:], 1.0), writes=[b_const])
        S.op("pool", lambda e: e.affine_select(maskb[:], maskb[:], [[-1, 128]], ALU.is_ge, 0.0, base=0, channel_multiplier=1),
             reads=[b_const], writes=[b_const])
        for dst, src, q in ((hgn, hgnw, "c0"), (mln, mlnw, "c1"), (cw, convw, "c2"), (cb, convb, "c3"), (gb, gateb, "c4")):
            S.dma("sp", q, dst[:], src, writes=[b_const])

        with ExitStack() as s0:
            lbt = sb("lbt", [128, 2, 2, 8], F32, s0)
            lbe = sb("lbe", [128, 2, 2, 8], F32, s0)
            lbs = sb("lbs", [128, 2, 8], F32, s0)
            lbv = sb("lbv", [128, 2, 8], F32, s0)
            bt = S.buf("lbtmp")
            S.dma("sp", "c5", lbt[:], hglb, writes=[bt])
            S.op("act", lambda e: e.activation(lbe[:], lbt[:], AF.Exp), reads=[bt], writes=[bt])
            S.op("dve", lambda e: e.tensor_tensor(lbs[:], lbe[:, :, 0, :], lbe[:, :, 1, :], ALU.add), reads=[bt], writes=[bt])
            S.op("dve", lambda e: e.reciprocal(lbs[:], lbs[:]), reads=[bt], writes=[bt])
            S.op("dve", lambda e: e.tensor_tensor(lbv[:], lbe[:, :, 0, :], lbs[:], ALU.mult), reads=[bt], writes=[bt])
            S.op("dve", lambda e: e.tensor_scalar(hgA[:], lbv[:], 0.5, 0.5, ALU.mult, ALU.add), reads=[bt], writes=[b_const])
            S.op("dve", lambda e: e.tensor_scalar(hgB[:], lbv[:], -0.5, 0.5, ALU.mult, ALU.add), reads=[bt], writes=[b_const])
            S.op("dve", lambda e: e.tensor_scalar(hgnBm[:], lbv[:], 0.5, -0.5, ALU.mult, ALU.add), reads=[bt], writes=[b_const])

        S.barrier()
        s_p1 = ExitStack()
        with ExitStack() as s0:
            cv = sb("cv", [128, 2, 16], F32, s_p1)
            csil = sb("csil", [128, 2, 16], F32, s_p1)
            csb = sb("csb", [128, 16, 2], BF16, s_p1)
            crep = sb("crep", [128, 16, 128], BF16, s_p1)
            bmb = sb("bmb", [128, 2048], F32, s_p1)
            bfm = sb("bfm", [128, 48], F32, s_p1)
            wmb = [sb(f"wmb{i}", [128, 16, 512], BF16, s_p1) for i in range(2)]
            rW = Rot(S, wmb, "wmb")
            bc = S.buf("cvt")
            S.dma("sp", "c6", cv[:], cvec, writes=[bc])
            S.dma("sp", "c7", bmb[:], bmod[:, 4096:6144].partition_broadcast(128), writes=[bc])
            S.dma("sp", "c8", bfm[:], bmodfm, writes=[bc])
            S.op("act", lambda e: e.activation(csil[:], cv[:], AF.Silu), reads=[bc], writes=[bc])
            S.op("dve", lambda e: e.tensor_copy(csb[:], csil[:].rearrange("p j c -> p c j")), reads=[bc], writes=[bc])
            for c in range(16):
                S.op("dve", lambda e: e.tensor_scalar(crep[:, c, :], maskf[:], 0.0, csil[:, 0, c:c + 1], ALU.mult, ALU.add),
                     reads=[bc, b_const], writes=[bc])
            for blk in range(12):
                wv, wbuf = rW.next()
                S.dma("pool", f"wm{blk % 2}", wv[:], wmod[blk], writes=[wbuf], max_dma_last_dim=4096)
                if blk < 8:
                    for q in range(4):
                        cbk = blk * 4 + q
                        pv, pb = rA.next()
                        for c in range(16):
                            S.op("pe", lambda e: e.matmul(pv[:, 0:2], wv[:, c, q * 128:(q + 1) * 128], csb[:, c, :], start=(c == 0), stop=(c == 15)),
                                 reads=[bc, wbuf], writes=[pb])
                        S.op("dve", lambda e: e.tensor_scalar(msf[:, cbk, :], pv[:, 0:2], bfm[:, cbk:cbk + 1], (1.0 if cbk >= 16 else 0.0), ALU.add, ALU.add),
                             reads=[pb, bc], writes=[b_mod])
                else:
                    pv, pb = rA.next()
                    for c in range(16):
                        S.op("pe", lambda e: e.matmul(pv[:], crep[:, c, :], wv[:, c, :], start=(c == 0), stop=(c == 15)),
                             reads=[bc, wbuf], writes=[pb])
                    S.op("dve", lambda e: e.tensor_tensor(gatex[:, (blk - 8) * 512:(blk - 7) * 512], pv[:], bmb[:, (blk - 8) * 512:(blk - 7) * 512], ALU.add),
                         reads=[pb, bc], writes=[b_mod])

        b_xm = S.buf("XM")
        with ExitStack() as s0:
            xin = [sb(f"xin{i}", [128, D], F32, s0) for i in range(3)]
            xmb = [sb(f"xmb{i}", [128, D], BF16, s0) for i in range(3)]
            xmt = [sb(f"xmt{i}", [128, 16, 128], BF16, s0) for i in range(2)]
            stt = [sb(f"stt{i}", [128, 4, 6], F32, s0) for i in range(3)]
            mv = [sb(f"mv{i}", [128, 4], F32, s0) for i in range(3)]
            rX, rXb, rXt, rSt, rMv = Rot(S, xin, "xin"), Rot(S, xmb, "xmb"), Rot(S, xmt, "xmt"), Rot(S, stt, "stt"), Rot(S, mv, "mv")
            xin_q = []

            def load_x(t):
                xv, xb_ = rX.next()
                S.dma("sp", f"xin{t % 3}", xv[:], xt[t * 128:(t + 1) * 128, :], writes=[xb_])
                xin_q.append((xv, xb_))

            load_x(0)
            load_x(1)
            pend = {}
            xt_chunk_bufs = {id(b_): S.bufs(16, "xtc") for b_ in rXt.bufs}

            def stage_a(t):
                if t + 2 < NT:
                    load_x(t + 2)
                xv, xb_ = xin_q.pop(0)
                sv, sb_ = rSt.next()
                for i in range(4):
                    S.op("dve", lambda e: e.bn_stats(sv[:, i, :], xv[:, i * 512:(i + 1) * 512]), reads=[xb_], writes=[sb_])
                mvv, mvb = rMv.next()
                S.op("dve", lambda e: e.bn_aggr(mvv[:, 0:2], sv[:].rearrange("p a b -> p (a b)")), reads=[sb_], writes=[mvb])
                S.op("act", lambda e: e.activation(mvv[:, 2:3], mvv[:, 1:2], AF.Ln, bias=LN_EPS, scale=1.0), reads=[mvb], writes=[mvb])
                S.op("act", lambda e: e.activation(mvv[:, 3:4], mvv[:, 2:3], AF.Exp, scale=-0.5), reads=[mvb], writes=[mvb])
                xbv, xbb = rXb.next()
                S.op("dve", lambda e: e.tensor_scalar(xbv[:], xv[:], mvv[:, 0:1], mvv[:, 3:4], ALU.subtract, ALU.mult), reads=[xb_, mvb], writes=[xbb])
                pend[t] = (xbv, xbb)

            def stage_b(t):
                xbv, xbb = pend.pop(t)
                jm = 1 if t < NT_CTX else 0
                xtv, xtb = rXt.next()
                cb_ = xt_chunk_bufs[id(xtb)]
                for g in range(2):
                    tv, tb = rT.next()
                    for i in range(8):
                        c = g * 8 + i
                        S.op("pe", lambda e: e.transpose(tv[:, i * 128:(i + 1) * 128], xbv[:, c * 128:(c + 1) * 128], ident[:]),
                             reads=[xbb, b_const], writes=[tb])
                    for i in range(8):
                        c = g * 8 + i
                        if g == 1:
                            S.op("dve", lambda e: e.tensor_scalar(xtv[:, c, :], tv[:, i * 128:(i + 1) * 128], msf[:, 16 + c, jm:jm + 1], msf[:, c, jm:jm + 1], ALU.mult, ALU.add),
                                 reads=[tb, b_mod], writes=[cb_[c]])
                        else:
                            S.op("act", lambda e: e.activation(xtv[:, c, :], tv[:, i * 128:(i + 1) * 128], AF.Identity, bias=msf[:, c, jm:jm + 1], scale=msf[:, 16 + c, jm:jm + 1]),
                                 reads=[tb, b_mod], writes=[cb_[c]])
                S.dma("sp", f"xmo{t % 2}", XM[t // 4][:, t % 4, :, :], xtv[:], reads=cb_, writes=[b_xm])

            stage_a(0)
            for t in range(NT):
                if t + 1 < NT:
                    stage_a(t + 1)
                stage_b(t)
        S.barrier()
        s_p1.close()
        if stop_after <= 1:
            S.wait_all("sp")
            return nc
        b_y = S.buf("Y")

        def inproj(blocks_spec, wtile, wbuf, xblk_rot, evac, nblk_of=None):
            nblk_of = nblk_of or {}
            nmax = max(nblk_of.get(sl_, NBLK) for sl_ in blocks_spec)
            for blk in range(nmax):
                ntok = min(512, T_ALL - blk * 512)
                xv, xb_ = xblk_rot.next()
                S.dma("sp", f"xblk{blk % 2}", xv[:, 0:ntok // 128, :, :], XM[blk][:, 0:ntok // 128, :, :], reads=[b_xm], writes=[xb_])
                for slot in blocks_spec:
                    if blk >= nblk_of.get(slot, NBLK):
                        continue
                    pv, pb = rA6g.next()
                    for c in range(16):
                        S.op("pe", lambda e: e.matmul(pv[:, 0:ntok], wtile[:, slot, c, :], xv[:, 0:ntok // 128, c, :], start=(c == 0), stop=(c == 15)),
                             reads=[wbuf, xb_], writes=[pb])
                    evac(slot, blk, ntok, pv, pb)

        S.barrier()
        with ExitStack() as s0:
            wt = sb("hw", [128, 5, 16, 128], BF16, s0)
            wtb = S.buf("hw")
            xblk = [sb(f"hx{i}", [128, 4, 16, 128], BF16, s0) for i in range(2)]
            rXB = Rot(S, xblk, "hx")
            Q = sb("hQ", [128, T_ALL], BF16, s0)
            Bc = [sb(f"hBc{d}", [128, T_ALL], F32, s0) for d in range(2)]
            Kd = [sb(f"hK{d}", [128, T_ALL], BF16, s0) for d in range(2)]
            Vf = sb("hV", [128, T_ALL], BF16, s0)
            Vt = sb("hVt", [128, NT, 128], BF16, s0)
            Z = sb("hZ", [128, T_OWN], BF16, s0)
            Of = sb("hOf", [128, NT_OWN, 128], BF16, s0)
            LFt = sb("hLF", [128, T_ALL], F32, s0)
            tmp = [sb(f"htmp{i}", [128, 512], F32, s0) for i in range(2)]
            rTmp = Rot(S, tmp, "htmp")
            small = sb("hsm", [128, NT, 6], F32, s0)
            St = sb("hS", [128, 128], F32, s0)
            mk = lambda nm, shp, dt, n=2: Rot(S, [sb(f"{nm}{i}", shp, dt, s0) for i in range(n)], nm)
            rSr, rOn, rYt, rKdt, rKt, rKD = (mk(nm, [128, 128], BF16, 3) for nm in ("hSr", "hon", "hyt", "hkdt", "hKt", "hKD"))
            rP = mk("hP", [128, 128], BF16, 6)
            rQt = mk("hQt", [128, 128], BF16, 8)
            rS4 = Rot(S, [banks[0][:, 0:128], banks[1][:, 0:128]], "hS", bufs=[bbuf[0], bbuf[1]])
            rD8 = Rot(S, [banks[2 + (k % 2)][:, (k // 2) * 128:(k // 2 + 1) * 128] for k in range(8)], "hD", bufs=[bbuf[2 + (k % 2)] for k in range(8)])
            rO8 = Rot(S, [banks[4][:, 0:128], banks[5][:, 0:128]], "hO", bufs=[bbuf[4], bbuf[5]])
            rTk = Rot(S, [TBs[0][:, 0:128]], "hTk", bufs=[tbuf[0]])
            rTf = Rot(S, [TBs[1][:, 0:128]], "hTf", bufs=[tbuf[1]])
            rTv = Rot(S, [TBs[0][:, 0:128], TBs[1][:, 0:128]], "hTv", bufs=[tbuf[0], tbuf[1]])
            rXa, rXr, rXt = (mk(nm, [128, 128], F32, 3) for nm in ("hXa", "hXr", "hXt"))
            rE1, rE2, rE3 = (mk(nm, [128, 128], BF16, 3) for nm in ("hE1", "hE2", "hE3"))
            rOs = mk("hos", [128, 128], F32, 7)
            rFin = mk("hfin", [128, 4], F32, 7)
            junk = sb("hjunk", [128, 128], F32, s0)
            bQ, bV, bVt, bZ, bOf, bLF, bS, bJ = (S.buf(n) for n in ("hQ", "hV", "hVt", "hZ", "hOf", "hLF", "hS", "hj"))
            bSmT = S.bufs(NT, "hsm")
            bBc = S.bufs(2, "hBc")
            bK = S.bufs(2, "hK")

            bVblk = S.bufs(NBLK, "hVblk")

            def v_transpose(blk):
                lo = blk * 512
                nt_ = min(512, T_ALL - lo) // 128
                tv, tb = rT.next()
                for j in range(nt_):
                    S.op("pe", lambda e: e.transpose(tv[:, j * 128:(j + 1) * 128], Vf[:, lo + j * 128:lo + (j + 1) * 128], ident[:]),
                         reads=[bVblk[blk], b_const], writes=[tb])
                S.op("act", lambda e: e.copy(Vt[:, blk * 4:blk * 4 + nt_, :], tv[:, 0:nt_ * 128].rearrange("p (a b) -> p a b", b=128)),
                     reads=[tb], writes=[bVt])

            for h in range(HG_H):
                for sl_ in range(5):
                    S.dma("pool", "hw", wt[:, sl_, :, :], win[h * 5 + sl_], writes=[wtb], max_dma_last_dim=4096)

                def evac(slot, blk, ntok, pv, pb, h=h):
                    lo = blk * 512
                    if slot == 0:
                        S.op("act", lambda e: e.activation(Q[:, lo:lo + ntok], pv[:, 0:ntok], AF.Silu), reads=[pb], writes=[bQ])
                    elif slot in (1, 2):
                        d = slot - 1
                        tv, tb = rTmp.next()
                        S.op("act", lambda e: e.activation(tv[:, 0:ntok], pv[:, 0:ntok], AF.Tanh, scale=0.5), reads=[pb], writes=[tb])
                        S.op("dve", lambda e: e.tensor_scalar(Kd[d][:, lo:lo + ntok], tv[:, 0:ntok], hgnBm[:, d, h:h + 1], hgB[:, d, h:h + 1], ALU.mult, ALU.add),
                             reads=[tb, b_const], writes=[bK[d]])
                        S.op("dve", lambda e: e.tensor_scalar(Bc[d][:, lo:lo + ntok], tv[:, 0:ntok], hgB[:, d, h:h + 1], hgA[:, d, h:h + 1], ALU.mult, ALU.add),
                             reads=[tb, b_const], writes=[bBc[d]])
                    elif slot == 3:
                        S.op("act", lambda e: e.copy(Vf[:, lo:lo + ntok], pv[:, 0:ntok]), reads=[pb], writes=[bVblk[blk]])
                        if blk >= 1:
                            v_transpose(blk - 1)
                    else:
                        a = max(lo, T_CTX)
                        e_ = min(lo + ntok, T_CTX + T_OWN)
                        if a < e_:
                            S.op("act", lambda e: e.activation(Z[:, a - T_CTX:e_ - T_CTX], pv[:, a - lo:e_ - lo], AF.Silu), reads=[pb], writes=[bZ])

                inproj(range(5), wt, wtb, rXB, evac, nblk_of={0: NB_OWN, 1: NB_OWN, 4: NB_OWN})
                v_transpose(NBLK - 1)

                for d in range(2):
                    TD = T_FWD if d == 0 else T_ALL
                    S.op("act", lambda e: e.activation(LFt[:, 0:TD], Bc[d][:, 0:TD], AF.Ln), reads=[bBc[d], bLF], writes=[bLF])
                    S.op("dve", lambda e: e.tensor_tensor_scan(Bc[d][:, 0:TD], LFt[:, 0:TD], LFt[:, 0:TD], 0.0, ALU.add, ALU.bypass), reads=[bLF, bBc[d]], writes=[bBc[d]])
                    ntd = (T_FWD // 128) if d == 0 else NT
                    B3 = Bc[d][:, 0:ntd * 128].rearrange("p (t j) -> p t j", j=128)
                    L3 = LFt[:, 0:ntd * 128].rearrange("p (t j) -> p t j", j=128)
                    S.op("dve", lambda e: e.tensor_tensor(small[:, 0:ntd, 3], B3[:, :, 127], B3[:, :, 0], ALU.subtract), reads=[bBc[d]] + bSmT, writes=bSmT)
                    S.op("dve", lambda e: e.tensor_tensor(small[:, 0:ntd, 3], small[:, 0:ntd, 3], L3[:, :, 0], ALU.add), reads=[bLF] + bSmT, writes=bSmT)
                    S.op("act", lambda e: e.activation(small[:, 0:ntd, 0], small[:, 0:ntd, 3], AF.Exp), reads=bSmT, writes=bSmT)
                    if d == 1:
                        S.op("dve", lambda e: e.tensor_tensor(Bc[1][:], Bc[1][:], LFt[:], ALU.subtract), reads=[bLF, bBc[1]], writes=[bBc[1]])
                    S.op("pool", lambda e: e.memset(St[:], 0.0), reads=[bS], writes=[bS])
                    mask = maskf if d == 0 else maskb
                    order = list(range(OWN1)) if d == 0 else ([1, 0] + list(range(NT - 1, OWN1 - 1, -1)) + list(range(OWN1 - 1, OWN0 - 1, -1)))
                    recs = {}

                    def m0(t, d=d):
                        sl = slice(t * 128, (t + 1) * 128)
                        e0, e1 = t * 128, t * 128 + 127
                        lat = OWN0 <= t < OWN1
                        rec = recs[t] = {"lat": lat, "sl": sl}
                        xav, xab = rXa.next()
                        if d == 0:
                            S.op("dve", lambda e: e.tensor_scalar(xav[:], Bc[0][:, sl], -1.0, Bc[0][:, e1:e1 + 1], ALU.mult, ALU.add), reads=[bBc[0]], writes=[xab])
                        else:
                            S.op("dve", lambda e: e.tensor_scalar(xav[:], Bc[1][:, sl], Bc[1][:, e0:e0 + 1], None, ALU.subtract), reads=[bBc[1]], writes=[xab])
                        rec.update(xav=xav, xab=xab)
                        if lat:
                            S.op("dve", lambda e: e.tensor_scalar(small[:, t, 4:5], xav[:, 64:65], -1.0, None, ALU.mult), reads=[bSmT[t], xab], writes=[bSmT[t]])
                            S.op("dve", lambda e: e.tensor_tensor(small[:, t, 2:3], small[:, t, 3:4], xav[:, 64:65], ALU.subtract), reads=[bSmT[t], xab], writes=[bSmT[t]])

                    def m1(t):
                        rec = recs[t]
                        ev, eb = rE1.next()
                        S.op("act", lambda e: e.activation(ev[:], rec["xav"][:], AF.Exp), reads=[rec["xab"]], writes=[eb])
                        rec.update(e1=ev, e1b=eb)
                        if rec["lat"]:
                            S.op("act", lambda e: e.activation(small[:, t, 1:2], small[:, t, 2:3], AF.Exp), reads=[bSmT[t]], writes=[bSmT[t]])
                            ev2, eb2 = rE2.next()
                            S.op("act", lambda e: e.activation(ev2[:], rec["xav"][:], AF.Exp, bias=rec["xav"][:, 64:65], scale=-1.0), reads=[rec["xab"]], writes=[eb2])
                            ev3, eb3 = rE3.next()
                            S.op("act", lambda e: e.activation(ev3[:], rec["xav"][:], AF.Exp, bias=small[:, t, 4:5], scale=1.0), reads=[rec["xab"], bSmT[t]], writes=[eb3])
                            rec.update(e2=ev2, e2b=eb2, e3=ev3, e3b=eb3)

                    def m2(t, d=d):
                        rec = recs[t]
                        sl = rec["sl"]
                        kdv, kdb = rKD.next()
                        S.op("dve", lambda e: e.tensor_tensor(kdv[:], Kd[d][:, sl], rec["e1"][:], ALU.mult), reads=[rec["e1b"], bK[d]], writes=[kdb])
                        rec.update(kdv=kdv, kdb=kdb)
                        if rec["lat"]:
                            qtv, qtb = rQt.next()
                            S.op("dve", lambda e: e.tensor_tensor(qtv[:], Q[:, sl], rec["e2"][:], ALU.mult), reads=[rec["e2b"], bQ], writes=[qtb])
                            ktv, ktb = rKt.next()
                            S.op("dve", lambda e: e.tensor_tensor(ktv[:], Kd[d][:, sl], rec["e3"][:], ALU.mult), reads=[rec["e3b"], bK[d]], writes=[ktb])
                            rec.update(qtv=qtv, qtb=qtb, ktv=ktv, ktb=ktb)

                    def m3(t):
                        rec = recs[t]
                        tv, tb = rTk.next()
                        S.op("pe", lambda e: e.transpose(tv[:, 0:128], rec["kdv"][:], ident[:]), reads=[rec["kdb"], b_const], writes=[tb])
                        rec.update(tv=tv, tb=tb)
                        if rec["lat"]:
                            sv, sbf = rS4.next()
                            S.op("pe", lambda e: e.matmul(sv, rec["ktv"][:], rec["qtv"][:], start=True, stop=True), reads=[rec["ktb"], rec["qtb"]], writes=[sbf])
                            rec.update(sv=sv, sbf=sbf)

                    def m4(t, mask=mask):
                        rec = recs[t]
                        kv, kb = rKdt.next()
                        S.op("act", lambda e: e.copy(kv[:], rec["tv"][:, 0:128]), reads=[rec["tb"]], writes=[kb])
                        rec.update(kv=kv, kb=kb)
                        if rec["lat"]:
                            pv, pbf = rP.next()
                            S.op("dve", lambda e: e.tensor_tensor(pv[:], rec["sv"], mask[:], ALU.mult), reads=[rec["sbf"], b_const], writes=[pbf])
                            rec.update(pv=pv, pbf=pbf)

                    def m5(t):
                        rec = recs[t]
                        dv, dbf = rD8.next()
                        S.op("pe", lambda e: e.matmul(dv, rec["kv"][:], Vt[:, t, :], start=True, stop=True), reads=[rec["kb"], bVt], writes=[dbf])
                        rec.update(dv=dv, dbf=dbf)

                    def m6(t):
                        rec = recs[t]
                        if rec["lat"]:
                            srv, srb = rSr.next()
                            S.op("act", lambda e: e.activation(srv[:], St[:], AF.Identity, scale=small[:, t, 1:2]), reads=[bS, bSmT[t]], writes=[srb])
                            rec.update(srv=srv, srb=srb)
                        S.op("dve", lambda e: e.scalar_tensor_tensor(St[:], St[:], small[:, t, 0:1], rec["dv"], ALU.mult, ALU.add),
                             reads=[bS, bSmT[t], rec["dbf"]], writes=[bS])

                    def m7(t, d=d):
                        rec = recs[t]
                        if not rec["lat"]:
                            return
                        ov, obf = rO8.next()
                        S.op("pe", lambda e: e.matmul(ov, rec["pv"][:], Vt[:, t, :], start=True, stop=False), reads=[rec["pbf"], bVt], writes=[obf])
                        S.op("pe", lambda e: e.matmul(ov, rec["qtv"][:], rec["srv"][:], start=False, stop=(d == 0)), reads=[rec["qtb"], rec["srb"]], writes=[obf])
                        if d == 1:
                            S.op("pe", lambda e: e.matmul(ov, ident[:], Of[:, t - NT_CTX, :], start=False, stop=True), reads=[b_const, bOf], writes=[obf])
                        rec.update(ov=ov, obf=obf)

                    def m8(t, d=d):
                        rec = recs[t]
                        if not rec["lat"]:
                            return
                        tl = t - NT_CTX
                        if d == 0:
                            S.op("act", lambda e: e.copy(Of[:, tl, :], rec["ov"]), reads=[rec["obf"]], writes=[bOf])
                            return
                        osv, osb = rOs.next()
                        S.op("act", lambda e: e.copy(osv[:], rec["ov"]), reads=[rec["obf"]], writes=[osb])
                        fv, fb = rFin.next()
                        rec.update(osv=osv, osb=osb, fv=fv, fb=fb)

                    def fin_stage(k):
                        def f(t, d=d, h=h):
                            rec = recs[t]
                            if not rec["lat"] or d == 0:
                                return
                            tl = t - NT_CTX
                            osv, osb, fv, fb = rec["osv"], rec["osb"], rec["fv"], rec["fb"]
                            if k == 0:
                                S.op("act", lambda e: e.activation(junk[:], osv[:], AF.Square, accum_out=fv[:, 0:1]), reads=[osb, bJ], writes=[fb, bJ])
                            elif k == 1:
                                S.op("act", lambda e: e.activation(fv[:, 1:2], fv[:, 0:1], AF.Ln, bias=NORM_EPS, scale=1.0 / 128.0), reads=[fb], writes=[fb])
                            elif k == 2:
                                S.op("act", lambda e: e.activation(fv[:, 2:3], fv[:, 1:2], AF.Exp, scale=-0.5), reads=[fb], writes=[fb])
                            elif k == 3:
                                onv, onbf = rOn.next()
                                S.op("dve", lambda e: e.tensor_scalar(onv[:], osv[:], fv[:, 2:3], None, ALU.mult), reads=[osb, fb], writes=[onbf])
                                rec.update(onv=onv, onbf=onbf)
                            elif k == 4:
                                tv, tb = rTf.next()
                                S.op("pe", lambda e: e.transpose(tv[:, 0:128], rec["onv"][:], ident[:]), reads=[rec["onbf"], b_const], writes=[tb])
                                rec.update(tv2=tv, tb2=tb)
                            else:
                                yv, ybf = rYt.next()
                                S.op("dve", lambda e: e.scalar_tensor_tensor(yv[:], rec["tv2"][:, 0:128], hgn[:, h:h + 1], Z[:, tl * 128:(tl + 1) * 128], ALU.mult, ALU.mult),
                                     reads=[rec["tb2"], b_const, bZ], writes=[ybf])
                                S.dma("sp", f"yo{tl % 3}", Y[tl][:, h, :], yv[:], reads=[ybf], writes=[b_y])
                        return f

                    stages = [m0, m1, m2, m3, m4, m5, m6, m7, m8] + [fin_stage(k) for k in range(6)]
                    nst = len(stages)
                    for i in range(len(order) + nst - 1):
                        for j in range(nst - 1, -1, -1):
                            if 0 <= i - j < len(order):
                                stages[j](order[i - j])
                        if i - (nst - 1) >= 0:
                            recs.pop(order[i - (nst - 1)], None)

        S.barrier()
        with ExitStack() as s0:
            wt = sb("mw", [128, 5, 16, 128], BF16, s0)
            wtb = S.buf("mw")
            xblk = [sb(f"mx{i}", [128, 4, 16, 128], BF16, s0) for i in range(2)]
            rXB = Rot(S, xblk, "mx")
            GA = sb("mGA", [48, T_ALL], F32, s0)
            GB = sb("mGB", [16, T_ALL], F32, s0)
            gsm = sb("mgsm", [16, NT, 4], F32, s0)
            gb48 = sb("mgb48", [48, 1], F32, s0)
            selb = sb("mselb", [16, 8, 128], F32, s0)
            selc = sb("mselc", [48, 48, 2], F32, s0)
            gcol = sb("mgcol", [128, NT, 2], F32, s0)
            etb = sb("metb", [128, NT], F32, s0)
            Qc = sb("mQc", [128, 2, T_FWD], BF16, s0)
            Kc = sb("mKc", [128, 2, T_ALL], BF16, s0)
            Vt = sb("mVt", [128, NT, 256], BF16, s0)
            Gg = sb("mGg", [128, 2, T_OWN], BF16, s0)
            Hf = sb("mHf", [128, NT_OWN, 256], BF16, s0)
            pre = sb("mpre", [128, 258 + 66 * 66], BF16, s0)
            dg = sb("mdg", [128, 9, 128], BF16, s0)
            pre2 = sb("mpre2", [128, 258 + 66 * 66], BF16, s0)
            rVtmp = Rot(S, [sb(f"mvtmp{i}", [128, 512], BF16, s0) for i in range(2)], "mvtmp")
            wg = rVtmp.views[0][:, 0:256].rearrange("p (a b) -> p a b", b=16)
            wg48 = dg[:].rearrange("p a b -> p (a b)")[:, 0:768].rearrange("p (a b) -> p a b", b=48)
            Ct = [sb(f"mC{c}", [128, 257], F32, s0) for c in range(2)]
            mk = lambda nm, shp, dt, n=2: Rot(S, [sb(f"{nm}{i}", shp, dt, s0) for i in range(n)], nm)
            LAG = 2
            rCd = [mk(f"mCd{c}_", [128, 257], BF16) for c in range(2)]
            rVe = mk("mVe", [128, 257], BF16, 7)
            rP = mk("mP", [128, 128], BF16, 3)
            rKt = mk("mkt", [128, 256], BF16, 5)
            rR4 = Rot(S, [banks[0][:, 0:128]], "mR", bufs=[bbuf[0]])
            rS4 = Rot(S, [banks[1][:, 0:128]], "mS", bufs=[bbuf[1]])
            rDc = [Rot(S, [banks[2 + c]], f"mDc{c}", bufs=[bbuf[2 + c]]) for c in range(2)]
            rO2 = Rot(S, [banks[4], banks[5]], "mO2", bufs=[bbuf[4], bbuf[5]])
            rG1 = rA
            rTk = Rot(S, [TBs[0][:, 0:256]], "mTk", bufs=[tbuf[0]])
            rTf = Rot(S, [TBs[1][:, 0:256]], "mTf", bufs=[tbuf[1]])
            rHn = mk("mhn", [128, 256], BF16, 3)
            rYt = mk("myt", [128, 2, 128], BF16, 3)
            rFin = mk("mfin", [128, 16], F32, 8)
            rOt = mk("mot", [128, 512], F32, 1)
            rHs = mk("mhs", [128, 256], BF16, 4)
            rRb = mk("mRb", [128, 128], F32, 3)
            rQt = mk("mQt", [128, 2, 128], BF16, 5)
            mlnh = sb("mlnh", [128, 8], F32, s0)
            bGA, bGB, bgsm, bsel, bgcol, betb, bQK, bVt, bGg, bHf, bpre, bdg, bmlnh, bpre2 = (S.buf(n) for n in
                ("mGA", "mGB", "mgsm", "msel", "mgcol", "metb", "mQK", "mVt", "mGg", "mHf", "mpre", "mdg", "mlnh", "mpre2"))
            bC = S.bufs(2, "mC")
            pres = [(Hf[:].rearrange("p a b -> p (a b)"), bHf), (Gg[:].rearrange("p a b -> p (a b)"), bGg), (pre[:], bpre), (pre2[:], bpre2)]
            QPRE = 258 + 66 * 38

            S.op("dve", lambda e: e.tensor_scalar(mlnh[:], mln[:], 0.5, None, ALU.mult), reads=[b_const], writes=[bmlnh])
            S.op("pool", lambda e: e.memset(pre[:], 0.0), writes=[bpre])
            S.op("pool", lambda e: e.memset(pre2[:], 0.0), writes=[bpre2])
            S.op("pool", lambda e: e.memset(selb[:], 1.0), writes=[bsel])
            S.op("pool", lambda e: e.affine_select(selb[:], selb[:], [[1, 8], [0, 128]], ALU.is_equal, 0.0, base=8, channel_multiplier=-1), reads=[bsel], writes=[bsel])
            S.op("pool", lambda e: e.memset(selc[:], 1.0), reads=[bsel], writes=[bsel])
            S.op("pool", lambda e: e.affine_select(selc[:], selc[:], [[1, 48], [0, 2]], ALU.is_equal, 0.0, base=0, channel_multiplier=-1), reads=[bsel], writes=[bsel])
            S.op("pool", lambda e: e.memset(gb48[:], 0.0), reads=[bsel], writes=[bsel])
            S.dma("sp", "c4", gb48[0:16, :], gateb, reads=[bsel], writes=[bsel])
            S.dma("sp", "c4", gb48[32:48, :], gateb, reads=[bsel], writes=[bsel])

            S.dma("pool", "mwg", wg, wgate, writes=[wtb])
            S.op("pool", lambda e: e.memset(wg48, 0.0), reads=[wtb], writes=[wtb])
            S.op("dve", lambda e: e.tensor_copy(wg48[:, :, 0:16], wg), reads=[wtb], writes=[wtb])
            S.op("dve", lambda e: e.tensor_copy(wg48[:, :, 32:48], wg), reads=[wtb], writes=[wtb])
            for blk in range(NBLK):
                lo = blk * 512
                ntok = min(512, T_ALL - lo)
                xv, xb_ = rXB.next()
                S.dma("sp", f"xblk{blk % 2}", xv[:, 0:ntok // 128, :, :], XM[blk][:, 0:ntok // 128, :, :], reads=[b_xm], writes=[xb_])
                pv, pb = rA.next()
                for c in range(16):
                    S.op("pe", lambda e: e.matmul(pv[0:48, 0:ntok], wg48[:, c, :], xv[:, 0:ntok // 128, c, :], start=(c == 0), stop=(c == 15)),
                         reads=[wtb, xb_], writes=[pb])
                S.op("act", lambda e: e.activation(GA[:, lo:lo + ntok], pv[0:48, 0:ntok], AF.Identity, bias=gb48[:, 0:1], scale=1.0), reads=[pb, bsel], writes=[bGA])
            S.op("act", lambda e: e.activation(GA[0:16, :], GA[0:16, :], AF.Exp, scale=-1.0), reads=[bGA], writes=[bGA])
            S.op("act", lambda e: e.activation(GA[0:16, :], GA[0:16, :], AF.Ln, bias=1.0, scale=1.0), reads=[bGA], writes=[bGA])
            S.op("dve", lambda e: e.tensor_scalar(GA[0:16, :], GA[0:16, :], -1.0, None, ALU.mult), reads=[bGA], writes=[bGA])
            S.op("dve", lambda e: e.tensor_tensor_scan(GB[:], GA[0:16, :], GA[0:16, :], 0.0, ALU.add, ALU.bypass), reads=[bGA], writes=[bGB])
            G3 = GB[:].rearrange("p (t j) -> p t j", j=128)
            A3 = GA[0:16, :].rearrange("p (t j) -> p t j", j=128)
            S.op("dve", lambda e: e.tensor_tensor(gsm[:, :, 0], G3[:, :, 127], G3[:, :, 0], ALU.subtract), reads=[bGB, bgsm], writes=[bgsm])
            S.op("dve", lambda e: e.tensor_tensor(gsm[:, :, 0], gsm[:, :, 0], A3[:, :, 0], ALU.add), reads=[bGA, bgsm], writes=[bgsm])
            S.op("dve", lambda e: e.tensor_copy(gsm[:, :, 2], G3[:, :, 127]), reads=[bGB, bgsm], writes=[bgsm])
            S.op("dve", lambda e: e.tensor_tensor(GA[0:16, :], GB[:], GA[0:16, :], ALU.subtract), reads=[bGA, bGB], writes=[bGA])
            S.op("dve", lambda e: e.tensor_copy(gsm[:, :, 1], A3[:, :, 0]), reads=[bGA, bgsm], writes=[bgsm])
            for t in range(NT):
                sl = slice(t * 128, (t + 1) * 128)
                S.op("dve", lambda e: e.tensor_scalar(GA[0:16, sl], GA[0:16, sl], gsm[:, t, 1:2], None, ALU.subtract), reads=[bGA, bgsm], writes=[bGA])
                S.op("dve", lambda e: e.tensor_scalar(GB[:, sl], GB[:, sl], -1.0, gsm[:, t, 2:3], ALU.mult, ALU.add), reads=[bGB, bgsm], writes=[bGB])

            S.barrier()

            def evac_v(blk, ntok, pv, pb, half):
                nt_ = ntok // 128
                vtv, vtb = rVtmp.next()
                S.op("act", lambda e: e.copy(vtv[:, 0:ntok], pv[:, 0:ntok]), reads=[pb], writes=[vtb])
                tv, tb = rT.next()
                for j in range(nt_):
                    S.op("pe", lambda e: e.transpose(tv[:, j * 128:(j + 1) * 128], vtv[:, j * 128:(j + 1) * 128], ident[:]), reads=[vtb, b_const], writes=[tb])
                S.op("act", lambda e: e.copy(Vt[:, blk * 4:blk * 4 + nt_, half * 128:(half + 1) * 128], tv[:, 0:ntok].rearrange("p (a b) -> p a b", b=128)),
                     reads=[tb], writes=[bVt])

            for h in range(ML_H):
                for half in range(2):
                    for sl_ in range(5):
                        S.dma("pool", "mw", wt[:, sl_, :, :], win[40 + h * 10 + half * 5 + sl_], writes=[wtb], max_dma_last_dim=4096)
                    if half == 0:
                        for k in (0, 1):
                            S.op("pool", lambda e: e.memset(pres[k][0][:, 0:QPRE], 0.0), reads=[pres[k][1]], writes=[pres[k][1]])

                        def evac(slot, blk, ntok, pv, pb, h=h):
                            lo = blk * 512
                            if slot == 4:
                                evac_v(blk, ntok, pv, pb, 0)
                                return
                            prv, prb = pres[slot]
                            if lo < T_CTX:
                                S.op("act", lambda e: e.copy(prv[:, 1:1 + T_CTX], pv[:, 0:T_CTX]), reads=[pb], writes=[prb])
                                a0, r0, nr = T_CTX, 0, 4
                            else:
                                a0, r0, nr = 0, (lo - T_CTX) // 64, ntok // 64
                            gsz = (QPRE - 258) if slot < 2 else 66 * 66
                            dst = prv[:, 258:258 + gsz].rearrange("p (r c) -> p r c", c=66)[:, 1 + r0:1 + r0 + nr, 1:65]
                            S.op("act", lambda e: e.copy(dst, pv[:, a0:a0 + nr * 64].rearrange("p (r c) -> p r c", c=64)), reads=[pb], writes=[prb])

                        inproj(range(5), wt, wtb, rXB, evac, nblk_of={0: NB_OWN, 1: NB_OWN})
                        for slot in range(4):
                            prv, prb = pres[slot]
                            ch = (2 * h + slot) if slot < 2 else (8 + 2 * h + slot - 2)
                            for tap in range(9):
                                S.op("dve", lambda e: e.tensor_scalar(dg[:, tap, :], ident[:], cw[:, ch, tap:tap + 1], None, ALU.mult), reads=[b_const, bdg], writes=[bdg])
                            pv, pb = rA6g.next()
                            for j in range(3):
                                S.op("pe", lambda e: e.matmul(pv[:, 0:T_CTX], dg[:, 3 + j, :], prv[:, j:j + T_CTX], start=(j == 0), stop=(j == 2)), reads=[bdg, prb], writes=[pb])
                            S.op("act", lambda e: e.activation((Qc[:, slot, 0:T_CTX] if slot < 2 else Kc[:, slot - 2, 0:T_CTX]), pv[:, 0:T_CTX], AF.Silu, bias=cb[:, ch:ch + 1], scale=1.0), reads=[pb, b_const], writes=[bQK])
                            gsz = (QPRE - 258) if slot < 2 else 66 * 66
                            grid = prv[:, 258:258 + gsz].rearrange("p (r c) -> p r c", c=66)
                            for rb8 in range(4 if slot < 2 else 8):
                                pv, pb = rA6g.next()
                                for tap in range(9):
                                    di, dj = tap // 3, tap % 3
                                    S.op("pe", lambda e: e.matmul(pv[:], dg[:, tap, :], grid[:, di + 8 * rb8:di + 8 * rb8 + 8, dj:dj + 64], start=(tap == 0), stop=(tap == 8)),
                                         reads=[bdg, prb], writes=[pb])
                                S.op("act", lambda e: e.activation((Qc[:, slot, T_CTX + rb8 * 512:T_CTX + (rb8 + 1) * 512] if slot < 2 else Kc[:, slot - 2, T_CTX + rb8 * 512:T_CTX + (rb8 + 1) * 512]), pv[:], AF.Silu, bias=cb[:, ch:ch + 1], scale=1.0),
                                     reads=[pb, b_const], writes=[bQK])
                    else:
                        def evac(slot, blk, ntok, pv, pb, h=h):
                            lo = blk * 512
                            if slot == 0:
                                evac_v(blk, ntok, pv, pb, 1)
                                return
                            a = max(lo, T_CTX)
                            e_ = min(lo + ntok, T_CTX + T_OWN)
                            if a >= e_:
                                return
                            n = e_ - a
                            c = (slot - 1) % 2
                            dst = Gg[:, c, a - T_CTX:a - T_CTX + n]
                            tv, tb = rOt.next()
                            if slot in (1, 2):
                                S.op("act", lambda e: e.activation(tv[:, 0:n], pv[:, a - lo:e_ - lo], AF.Tanh, scale=0.5), reads=[pb], writes=[tb])
                                S.op("dve", lambda e: e.tensor_scalar(dst, tv[:, 0:n], 1.0, None, ALU.add), reads=[tb, bGg], writes=[bGg])
                            else:
                                S.op("act", lambda e: e.activation(tv[:, 0:n], pv[:, a - lo:e_ - lo], AF.Silu), reads=[pb], writes=[tb])
                                S.op("dve", lambda e: e.tensor_tensor(dst, dst, tv[:, 0:n], ALU.mult), reads=[tb, bGg], writes=[bGg])

                        inproj(range(5), wt, wtb, rXB, evac, nblk_of={1: NB_OWN, 2: NB_OWN, 3: NB_OWN, 4: NB_OWN})

                for d in range(2):
                    frow = 8 + d * 4 + h
                    irow = 32 + d * 4 + h
                    Xarr, bX = (GB, bGB) if d == 0 else (GA, bGA)
                    pv, pb = rG1.next()
                    S.op("pe", lambda e: e.matmul(pv[:, 0:NT], selb[:, frow - 8, :], gsm[:, :, 0], start=True, stop=True), reads=[bsel, bgsm], writes=[pb])
                    S.op("act", lambda e: e.activation(etb[:], pv[:, 0:NT], AF.Exp), reads=[pb], writes=[betb])
                    pv, pb = rG1.next()
                    for t in range(NT):
                        sl = slice(t * 128, (t + 1) * 128)
                        if d == 0:
                            S.op("pe", lambda e: e.matmul(pv[:, 2 * t:2 * t + 2], GB[:, sl], selc[0:16, frow, :], start=True, stop=False), reads=[bGB, bsel], writes=[pb])
                        else:
                            S.op("pe", lambda e: e.matmul(pv[:, 2 * t:2 * t + 2], GA[:, sl], selc[:, frow, :], start=True, stop=False), reads=[bGA, bsel], writes=[pb])
                        S.op("pe", lambda e: e.matmul(pv[:, 2 * t:2 * t + 2], GA[:, sl], selc[:, irow, :], start=False, stop=True), reads=[bGA, bsel], writes=[pb])
                    S.op("act", lambda e: e.activation(gcol[:].rearrange("p a b -> p (a b)"), pv[:, 0:2 * NT], AF.Exp), reads=[pb], writes=[bgcol])

                    for c in range(2):
                        S.op("pool", lambda e: e.memset(Ct[c][:], 0.0), reads=[bC[c]], writes=[bC[c]])
                    mask = maskf if d == 0 else maskb
                    order = list(range(OWN1)) if d == 0 else ([1, 0] + list(range(NT - 1, OWN1 - 1, -1)) + list(range(OWN1 - 1, OWN0 - 1, -1)))
                    recs = {}

                    def m0(t, mask=mask):
                        sl = slice(t * 128, (t + 1) * 128)
                        lat = OWN0 <= t < OWN1
                        rec = recs[t] = {"lat": lat, "sl": sl}
                        vev, veb = rVe.next()
                        S.op("act", lambda e: e.activation(vev[:, 0:256], Vt[:, t, :], AF.Identity, scale=gcol[:, t, 0:1]), reads=[bVt, bgcol], writes=[veb])
                        S.op("act", lambda e: e.copy(vev[:, 256:257], gcol[:, t, 0:1]), reads=[bgcol, veb], writes=[veb])
                        tv, tb = rTk.next()
                        for c in range(2):
                            S.op("pe", lambda e: e.transpose(tv[:, c * 128:(c + 1) * 128], Kc[:, c, sl], ident[:]), reads=[bQK, b_const], writes=[tb])
                        rec.update(vev=vev, veb=veb, tv=tv, tb=tb)
                        if lat:
                            rv, rb_ = rR4.next()
                            S.op("pe", lambda e: e.matmul(rv, selb[:, frow - 8, :], Xarr[0:16, sl], start=True, stop=True), reads=[bsel, bX], writes=[rb_])
                            rec.update(rv=rv, rb_=rb_)

                    def m1(t):
                        rec = recs[t]
                        kv, kb = rKt.next()
                        S.op("act", lambda e: e.copy(kv[:], rec["tv"][:, 0:256]), reads=[rec["tb"]], writes=[kb])
                        rec.update(kv=kv, kb=kb)
                        if rec["lat"]:
                            rbv, rbb = rRb.next()
                            S.op("act", lambda e: e.activation(rbv[:], rec["rv"], AF.Exp, bias=-math.log(16.0), scale=-1.0), reads=[rec["rb_"]], writes=[rbb])
                            rec.update(rbv=rbv, rbb=rbb)

                    def m2(t):
                        rec = recs[t]
                        if not rec["lat"]:
                            return
                        qtv, qtb = rQt.next()
                        for c in range(2):
                            S.op("dve", lambda e: e.tensor_tensor(qtv[:, c, :], Qc[:, c, rec["sl"]], rec["rbv"][:], ALU.mult), reads=[bQK, rec["rbb"]], writes=[qtb])
                        rec.update(qtv=qtv, qtb=qtb)

                    def m3(t):
                        rec = recs[t]
                        if not rec["lat"]:
                            return
                        sv, sbf = rS4.next()
                        for c in range(2):
                            S.op("pe", lambda e: e.matmul(sv, Kc[:, c, rec["sl"]], rec["qtv"][:, c, :], start=(c == 0), stop=(c == 1)), reads=[bQK, rec["qtb"]], writes=[sbf])
                        rec.update(sv=sv, sbf=sbf)

                    def m4(t, mask=mask):
                        rec = recs[t]
                        if not rec["lat"]:
                            return
                        ppv, pbf = rP.next()
                        S.op("dve", lambda e: e.tensor_tensor(ppv[:], rec["sv"], mask[:], ALU.mult), reads=[rec["sbf"], b_const], writes=[pbf])
                        rec.update(ppv=ppv, pbf=pbf)

                    def m5(t):
                        rec = recs[t]
                        if rec["lat"]:
                            cdv = []
                            for c in range(2):
                                v_, b_ = rCd[c].next()
                                S.op("act", lambda e: e.activation(v_[:], Ct[c][:], AF.Identity, scale=etb[:, t:t + 1]), reads=[bC[c], betb], writes=[b_])
                                cdv.append((v_, b_))
                            rec.update(cdv=cdv)
                        dcs = []
                        for c in range(2):
                            dv, dbf = rDc[c].next()
                            S.op("pe", lambda e: e.matmul(dv[:, 0:257], rec["kv"][:, c * 128:(c + 1) * 128], rec["vev"][:], start=True, stop=True), reads=[rec["kb"], rec["veb"]], writes=[dbf])
                            dcs.append((dv, dbf))
                        rec.update(dcs=dcs)

                    def m6(t):
                        rec = recs[t]
                        for c in range(2):
                            dv, dbf = rec["dcs"][c]
                            S.op("dve", lambda e: e.scalar_tensor_tensor(Ct[c][:], Ct[c][:], etb[:, t:t + 1], dv[:, 0:257], ALU.mult, ALU.add),
                                 reads=[bC[c], betb, dbf], writes=[bC[c]])
                        if rec["lat"]:
                            ov, obf = rO2.next()
                            S.op("pe", lambda e: e.matmul(ov[:, 0:257], rec["ppv"][:], rec["vev"][:], start=True, stop=False), reads=[rec["pbf"], rec["veb"]], writes=[obf])
                            for c in range(2):
                                S.op("pe", lambda e: e.matmul(ov[:, 0:257], rec["qtv"][:, c, :], rec["cdv"][c][0][:], start=False, stop=(c == 1)),
                                     reads=[rec["qtb"], rec["cdv"][c][1]], writes=[obf])
                            rec.update(ov=ov, obf=obf)

                    def m7(t):
                        rec = recs[t]
                        if not rec["lat"]:
                            return
                        fv, fb = rFin.next()
                        S.op("act", lambda e: e.activation(fv[:, 12:13], rec["ov"][:, 256:257], AF.Abs), reads=[rec["obf"]], writes=[fb])
                        rec.update(fv=fv, fb=fb)

                    def m8(t, d=d):
                        rec = recs[t]
                        if not rec["lat"]:
                            return
                        tl = t - NT_CTX
                        fv, fb, ov, obf = rec["fv"], rec["fb"], rec["ov"], rec["obf"]
                        S.op("dve", lambda e: e.tensor_scalar(fv[:, 0:1], fv[:, 12:13], 1.0, None, ALU.max), reads=[fb], writes=[fb])
                        S.op("dve", lambda e: e.reciprocal(fv[:, 1:2], fv[:, 0:1]), reads=[fb], writes=[fb])
                        if d == 0:
                            S.op("dve", lambda e: e.tensor_scalar(Hf[:, tl, :], ov[:, 0:256], fv[:, 1:2], None, ALU.mult), reads=[obf, fb], writes=[bHf])
                            return
                        hv, hb = rHs.next()
                        S.op("dve", lambda e: e.scalar_tensor_tensor(hv[:], ov[:, 0:256], fv[:, 1:2], Hf[:, tl, :], ALU.mult, ALU.add), reads=[obf, fb, bHf], writes=[hb])
                        rec.update(hv=hv, hb=hb)

                    def fin_stage(k):
                        def f(t, d=d, h=h):
                            rec = recs[t]
                            if not rec["lat"] or d == 0:
                                return
                            tl = t - NT_CTX
                            fv, fb, hv, hb = rec["fv"], rec["fb"], rec["hv"], rec["hb"]
                            if k == 0:
                                S.op("dve", lambda e: e.bn_stats(fv[:, 2:8], hv[:]), reads=[hb, fb], writes=[fb])
                                S.op("dve", lambda e: e.bn_aggr(fv[:, 8:10], fv[:, 2:8]), reads=[fb], writes=[fb])
                            elif k == 1:
                                S.op("act", lambda e: e.activation(fv[:, 10:11], fv[:, 9:10], AF.Ln, bias=NORM_EPS, scale=1.0), reads=[fb], writes=[fb])
                                S.op("act", lambda e: e.activation(fv[:, 11:12], fv[:, 10:11], AF.Exp, scale=-0.5), reads=[fb], writes=[fb])
                            elif k == 2:
                                hnv, hnb = rHn.next()
                                S.op("dve", lambda e: e.tensor_scalar(hnv[:], hv[:], fv[:, 8:9], fv[:, 11:12], ALU.subtract, ALU.mult), reads=[hb, fb], writes=[hnb])
                                rec.update(hnv=hnv, hnb=hnb)
                            elif k == 3:
                                tv, tb = rTf.next()
                                for c in range(2):
                                    S.op("pe", lambda e: e.transpose(tv[:, c * 128:(c + 1) * 128], rec["hnv"][:, c * 128:(c + 1) * 128], ident[:]), reads=[rec["hnb"], b_const], writes=[tb])
                                rec.update(tv2=tv, tb2=tb)
                            else:
                                yv, ybf = rYt.next()
                                for c in range(2):
                                    S.op("dve", lambda e: e.scalar_tensor_tensor(yv[:, c, :], rec["tv2"][:, c * 128:(c + 1) * 128], mlnh[:, 2 * h + c:2 * h + c + 1],
                                                                                Gg[:, c, tl * 128:(tl + 1) * 128], ALU.mult, ALU.mult),
                                         reads=[rec["tb2"], bmlnh, bGg], writes=[ybf])
                                S.dma("sp", f"yo{tl % 3}", Y[tl][:, 8 + 2 * h:8 + 2 * h + 2, :], yv[:], reads=[ybf], writes=[b_y])
                        return f

                    stages = [m0, m1, m2, m3, m4, m5, m6, m7, m8] + [fin_stage(k) for k in range(5)]
                    nst = len(stages)
                    for i in range(len(order) + nst - 1):
                        for j in range(nst - 1, -1, -1):
                            if 0 <= i - j < len(order):
                                stages[j](order[i - j])
                        if i - (nst - 1) >= 0:
                            recs.pop(order[i - (nst - 1)], None)

        S.barrier()
        with ExitStack() as s0:
            wo = sb("wo", [128, 16, D], BF16, s0)
            lg = sb("lg", [128, D], F32, s0)
            lbb = sb("lbb", [128, D], F32, s0)
            bw = S.buf("wo")
            rA6 = Rot(S, banks, "pA6", bufs=bbuf)
            for c4 in range(4):
                S.dma("pool", "wo", wo[:, c4 * 4:(c4 + 1) * 4, :], wout[:, c4 * 4:(c4 + 1) * 4, :], writes=[bw], max_dma_last_dim=4096)
            S.dma("sp", "c6", lg[:], lng.partition_broadcast(128), writes=[bw])
            S.dma("sp", "c7", lbb[:], lnb.partition_broadcast(128), writes=[bw])
            yb = [sb(f"yb{i}", [128, 16, 128], BF16, s0) for i in range(3)]
            xr = [sb(f"xr{i}", [128, D], F32, s0) for i in range(3)]
            ob = [sb(f"ob{i}", [128, D], F32, s0) for i in range(2)]
            stt = [sb(f"ost{i}", [128, 4, 6], F32, s0) for i in range(2)]
            mv = [sb(f"omv{i}", [128, 4], F32, s0) for i in range(2)]
            rYb, rXr, rOb, rSt, rMv = Rot(S, yb, "yb"), Rot(S, xr, "xr"), Rot(S, ob, "ob"), Rot(S, stt, "ost"), Rot(S, mv, "omv")
            ld_q = []

            def load3(tl):
                yv, ybf = rYb.next()
                S.dma("sp", f"yi{tl % 3}", yv[:], Y[tl], reads=[b_y], writes=[ybf])
                xv, xbf = rXr.next()
                S.dma("sp", f"xr{tl % 3}", xv[:], xt[T_CTX + tl * 128:T_CTX + (tl + 1) * 128, :], writes=[xbf])
                ld_q.append((yv, ybf, xv, xbf))

            NT3 = NT_OWN
            load3(0)
            load3(1)
            for tl in range(NT3):
                if tl + 2 < NT3:
                    load3(tl + 2)
                yv, ybf, xv, xbf = ld_q.pop(0)
                ov, obf = rOb.next()
                for nb in range(4):
                    pv, pb = rA6.next()
                    for c in range(16):
                        S.op("pe", lambda e: e.matmul(pv[:], yv[:, c, :], wo[:, c, nb * 512:(nb + 1) * 512], start=(c == 0), stop=(c == 15)),
                             reads=[ybf, bw], writes=[pb])
                    S.op("dve", lambda e: e.tensor_tensor(ov[:, nb * 512:(nb + 1) * 512], pv[:], gatex[:, nb * 512:(nb + 1) * 512], ALU.mult),
                         reads=[pb, b_mod], writes=[obf])
                S.op("dve", lambda e: e.scalar_tensor_tensor(ov[:], xv[:], ALPHA, ov[:], ALU.mult, ALU.add), reads=[xbf, obf], writes=[obf])
                sv, sbf = rSt.next()
                for i in range(4):
                    S.op("dve", lambda e: e.bn_stats(sv[:, i, :], ov[:, i * 512:(i + 1) * 512]), reads=[obf], writes=[sbf])
                mvv, mvb = rMv.next()
                S.op("dve", lambda e: e.bn_aggr(mvv[:, 0:2], sv[:].rearrange("p a b -> p (a b)")), reads=[sbf], writes=[mvb])
                S.op("act", lambda e: e.activation(mvv[:, 2:3], mvv[:, 1:2], AF.Ln, bias=LN_EPS, scale=1.0), reads=[mvb], writes=[mvb])
                S.op("act", lambda e: e.activation(mvv[:, 3:4], mvv[:, 2:3], AF.Exp, scale=-0.5), reads=[mvb], writes=[mvb])
                S.op("dve", lambda e: e.tensor_scalar(ov[:], ov[:], mvv[:, 0:1], mvv[:, 3:4], ALU.subtract, ALU.mult), reads=[obf, mvb], writes=[obf])
                S.op("dve", lambda e: e.tensor_tensor(ov[:], ov[:], lg[:], ALU.mult), reads=[obf, bw], writes=[obf])
                S.op("dve", lambda e: e.tensor_tensor(ov[:], ov[:], lbb[:], ALU.add), reads=[obf, bw], writes=[obf])
                S.dma("sp", f"oo{tl % 2}", out[tl * 128:(tl + 1) * 128, :], ov[:], reads=[obf], writes=[])
        S.wait_all("sp")
    return nc


_PROGRAM = None


def _layout_inputs(x, c, ctx, c_ctx, w_mod, b_mod, w_in, conv_w, conv_b, hg_lb, ml_gate_b,
                   hg_norm_w, ml_norm_w, w_out, ln_g, ln_b):
    f = lambda a: np.ascontiguousarray(np.asarray(a, dtype=np.float32))
    x, c, ctx, c_ctx = f(x), f(c), f(ctx), f(c_ctx)
    w_mod, b_mod, w_in = f(w_mod)[0], f(b_mod)[0], f(w_in)[0]
    conv_w, conv_b, hg_lb = f(conv_w)[0], f(conv_b)[0], f(hg_lb)
    gate_b, hgn, mln, w_out = f(ml_gate_b)[0], f(hg_norm_w)[0], f(ml_norm_w)[0], f(w_out)[0]
    ln_g, ln_b = f(ln_g)[0], f(ln_b)[0]
    WA = 1024
    blocks = []
    for h in range(HG_H):
        for g in range(5):
            blocks.append(np.arange(g * WA + h * 128, g * WA + (h + 1) * 128))
    base = 5 * WA
    for h in range(ML_H):
        qs = base + h * 256
        ks = base + 1024 + h * 256
        vs = base + 2048 + h * 256
        os_ = base + 3072 + h * 256
        zs = base + 4096 + h * 256
        for s in (qs, qs + 128, ks, ks + 128, vs, vs + 128, os_, os_ + 128, zs, zs + 128):
            blocks.append(np.arange(s, s + 128))
    cols = np.concatenate(blocks)
    wsel = w_in[:, cols]
    win = np.ascontiguousarray(wsel.reshape(16, 128, 80, 128).transpose(2, 1, 0, 3))
    wg = w_in[:, base + 5 * 1024:base + 5 * 1024 + 16]
    wgate = np.ascontiguousarray(wg.reshape(16, 128, 16).transpose(1, 0, 2))
    wmod = np.ascontiguousarray(w_mod.reshape(16, 128, 12, 512).transpose(2, 1, 0, 3))
    bmod = np.ascontiguousarray(b_mod.reshape(1, 6144))
    bmodfm = np.ascontiguousarray(b_mod.reshape(48, 128).T)
    convw = np.ascontiguousarray(conv_w.reshape(9, 16, 128).transpose(2, 1, 0))
    convb = np.ascontiguousarray(conv_b.reshape(16, 128).T)
    lower_in = hg_lb[:, :, :]
    hglb = np.ascontiguousarray(lower_in.reshape(2, 2, 8, 128).transpose(3, 0, 1, 2))
    gateb = np.ascontiguousarray(gate_b.reshape(16, 1))
    hgnw = np.ascontiguousarray(hgn.reshape(8, 128).T)
    mlnw = np.ascontiguousarray(mln.reshape(8, 128).T)
    wout = np.ascontiguousarray(w_out.reshape(16, 128, D).transpose(1, 0, 2))
    lng = np.ascontiguousarray(ln_g.reshape(1, D))
    lnb = np.ascontiguousarray(ln_b.reshape(1, D))
    def swap_pairs(blocks_list):
        out_ = list(blocks_list)
        for h in range(HG_H):
            out_[h * 5 + 1], out_[h * 5 + 2] = out_[h * 5 + 2], out_[h * 5 + 1]
        return out_
    win_m = [win, np.ascontiguousarray(win[swap_pairs(list(range(80)))])]
    gperm = np.array([4, 5, 6, 7, 0, 1, 2, 3, 12, 13, 14, 15, 8, 9, 10, 11])
    wgate_m = [wgate, np.ascontiguousarray(wgate[:, :, gperm])]
    gateb_m = [gateb, np.ascontiguousarray(gateb[gperm])]
    hglb_m = [hglb, np.ascontiguousarray(hglb[:, ::-1])]
    convw_m = [convw, np.ascontiguousarray(convw[:, :, ::-1])]
    maps = []
    for core in range(N_CORES):
        b, m = core // 2, core % 2
        if m == 0:
            xt = np.concatenate([ctx[b], x[b, :T_OWN], x[b, T_OWN:]], axis=0)
        else:
            xt = np.concatenate([ctx[b][::-1], x[b, T_OWN:][::-1], x[b, :T_OWN][::-1]], axis=0)
        xt = np.ascontiguousarray(xt)
        cv = np.stack([c[b], c_ctx], axis=0)
        cvec = np.ascontiguousarray(cv.reshape(2, 16, 128).transpose(2, 0, 1))
        maps.append({"xt": xt, "cvec": cvec, "wmod": wmod, "bmod": bmod, "bmodfm": bmodfm, "win": win_m[m], "wgate": wgate_m[m],
                     "convw": convw_m[m], "convb": convb, "hglb": hglb_m[m], "gateb": gateb_m[m], "hgnw": hgnw, "mlnw": mlnw,
                     "wout": wout, "lng": lng, "lnb": lnb})
    return maps


def kernel(**inputs):
    global _PROGRAM
    if _PROGRAM is None:
        _PROGRAM = build_program()
    maps = _layout_inputs(**inputs)
    res = run_bass_kernel_spmd(_PROGRAM, maps, core_ids=list(range(N_CORES)))
    full = np.empty((4, T_LAT, D), dtype=np.float32)
    for core in range(N_CORES):
        b, m = core // 2, core % 2
        o = np.asarray(res.results[core]["out"], dtype=np.float32)
        if m == 0:
            full[b, :T_OWN] = o
        else:
            full[b, T_OWN:] = o[::-1]
    return full
```
